# Optimizing a Trainium2 kernel written in Bass

```python
import jax, jax.numpy as jnp
from jax import lax
import numpy as np

D_MODEL = 1024
BATCH = 4
SEQ = 8192
DEPTH = 2
DEC_BATCH = 8
DEC_SEQ = 64
PAST_LEN = 4096

CHUNK = 64
N_MIXERS = 2
N_CONV_LAYERS = (DEPTH + 1) // 2
N_MLA_LAYERS = DEPTH // 2
CONV_WIDTH = 3
CONV_DIM = D_MODEL
N_HEADS = 16
QK_NOPE_DIM = 64
QK_ROPE_DIM = 32
V_HEAD_DIM = 64
QK_DIM = QK_NOPE_DIM + QK_ROPE_DIM
Q_LORA_RANK = 256
KV_LORA_RANK = 128
MLA_GATE_DIM = N_HEADS * V_HEAD_DIM
MLA_IN_DIM = Q_LORA_RANK + KV_LORA_RANK + QK_ROPE_DIM + MLA_GATE_DIM
ROPE_THETA = 10000.0
Q_BLOCK = 128
EPS = 1e-6
SM_SCALE = QK_DIM ** -0.5
NEG_INF = -1e30

kernel_name = "hybrid_shortconv_mla_stream_step"


def rms_norm(x, g):
    xf = x.astype(jnp.float32)
    y = xf * lax.rsqrt(jnp.mean(xf * xf, axis=-1, keepdims=True) + EPS)
    return (y * g.astype(jnp.float32)).astype(x.dtype)


def rope_cos_sin(pos):
    inv = 1.0 / (ROPE_THETA ** (jnp.arange(0, QK_ROPE_DIM, 2, dtype=jnp.float32) / QK_ROPE_DIM))
    ang = pos.astype(jnp.float32)[:, None] * inv[None, :]
    return jnp.cos(ang), jnp.sin(ang)


def apply_rope(x, cos, sin):
    xf = x.astype(jnp.float32)
    x1, x2 = xf[..., :QK_ROPE_DIM // 2], xf[..., QK_ROPE_DIM // 2:]
    out = jnp.concatenate([x1 * cos - x2 * sin, x2 * cos + x1 * sin], axis=-1)
    return out.astype(x.dtype)


def short_conv_mixer(h, conv_prev, w_in, conv_w, w_out):
    T = h.shape[1]
    b_gate, c_gate, xv, z = jnp.split(h @ w_in, 4, axis=-1)
    u = c_gate * xv
    up = jnp.concatenate([conv_prev.astype(u.dtype), u], axis=1)
    conv = up[:, 0:T] * conv_w[0]
    for k in range(1, CONV_WIDTH):
        conv = conv + up[:, k:k + T] * conv_w[k]
    y = (jax.nn.silu(z) * b_gate * conv) @ w_out
    return y, up[:, -(CONV_WIDTH - 1):]


def mla_project(h, pos, w_in, q_norm_g, w_qb, kv_norm_g):
    B, T, _ = h.shape
    p = h @ w_in
    q_lat = p[..., :Q_LORA_RANK]
    kv_lat = p[..., Q_LORA_RANK:Q_LORA_RANK + KV_LORA_RANK]
    k_rope = p[..., Q_LORA_RANK + KV_LORA_RANK:Q_LORA_RANK + KV_LORA_RANK + QK_ROPE_DIM]
    z = p[..., Q_LORA_RANK + KV_LORA_RANK + QK_ROPE_DIM:]
    q = (rms_norm(q_lat, q_norm_g) @ w_qb).reshape(B, T, N_HEADS, QK_DIM)
    cos, sin = rope_cos_sin(pos)
    q_rope = apply_rope(q[..., QK_NOPE_DIM:], cos[:, None, :], sin[:, None, :])
    q = jnp.concatenate([q[..., :QK_NOPE_DIM], q_rope], axis=-1)
    k_rope = apply_rope(k_rope, cos, sin)
    c_kv = rms_norm(kv_lat, kv_norm_g)
    return q, c_kv, k_rope, z


def mla_expand_kv(c_kv, k_rope, w_kvb):
    B, S, _ = c_kv.shape
    kv = (c_kv @ w_kvb).reshape(B, S, N_HEADS, QK_NOPE_DIM + V_HEAD_DIM)
    k_pe = jnp.broadcast_to(k_rope[:, :, None, :].astype(kv.dtype), (B, S, N_HEADS, QK_ROPE_DIM))
    k = jnp.concatenate([kv[..., :QK_NOPE_DIM], k_pe], axis=-1)
    v = kv[..., QK_NOPE_DIM:]
    return k, v


def attend(q, k, v, mask):
    s = jnp.einsum('bqhd,bkhd->bhqk', q, k).astype(jnp.float32) * SM_SCALE
    if mask is not None:
        s = jnp.where(mask[None, None], s, NEG_INF)
    p = jax.nn.softmax(s, axis=-1).astype(v.dtype)
    return jnp.einsum('bhqk,bkhd->bqhd', p, v)


def mla_prompt_attention(q, k, v):
    B, S = q.shape[0], q.shape[1]
    k_chunk = jnp.arange(S) // CHUNK

    def block(i):
        start = i * Q_BLOCK
        qb = lax.dynamic_slice_in_dim(q, start, Q_BLOCK, axis=1)
        q_chunk = (start + jnp.arange(Q_BLOCK)) // CHUNK
        mask = k_chunk[None, :] <= q_chunk[:, None]
        return attend(qb, k, v, mask)

    o = lax.map(block, jnp.arange(S // Q_BLOCK))
    return jnp.moveaxis(o, 0, 1).reshape(B, S, N_HEADS * V_HEAD_DIM)


def setup_inputs(seed: int = 0) -> dict:
    key = jax.random.key(seed)
    ks = jax.random.split(key, 20)
    f32 = jnp.float32

    def nrm(k, shape, scale):
        return jax.random.normal(k, shape, f32) * scale

    return {
        "x_prompt": nrm(ks[0], (BATCH, SEQ, D_MODEL), 1.0),
        "x_sample": nrm(ks[1], (DEC_BATCH, DEC_SEQ, D_MODEL), 1.0),
        "state_conv": nrm(ks[2], (N_CONV_LAYERS, DEC_BATCH, CONV_WIDTH - 1, CONV_DIM), 1.0),
        "cache_ckv": nrm(ks[3], (N_MLA_LAYERS, DEC_BATCH, PAST_LEN, KV_LORA_RANK), 1.0),
        "cache_krope": nrm(ks[4], (N_MLA_LAYERS, DEC_BATCH, PAST_LEN, QK_ROPE_DIM), 1.0),
        "norm_g": 1.0 + nrm(ks[5], (DEPTH, D_MODEL), 0.05),
        "final_norm_g": 1.0 + nrm(ks[6], (D_MODEL,), 0.05),
        "conv_w_in": nrm(ks[7], (N_CONV_LAYERS, D_MODEL, 4 * CONV_DIM), D_MODEL ** -0.5),
        "conv_w": nrm(ks[8], (N_CONV_LAYERS, CONV_WIDTH, CONV_DIM), CONV_WIDTH ** -0.5),
        "conv_w_out": nrm(ks[9], (N_CONV_LAYERS, CONV_DIM, D_MODEL), CONV_DIM ** -0.5),
        "mla_w_in": nrm(ks[10], (N_MLA_LAYERS, D_MODEL, MLA_IN_DIM), D_MODEL ** -0.5),
        "mla_q_norm_g": 1.0 + nrm(ks[11], (N_MLA_LAYERS, Q_LORA_RANK), 0.05),
        "mla_w_qb": nrm(ks[12], (N_MLA_LAYERS, Q_LORA_RANK, N_HEADS * QK_DIM), Q_LORA_RANK ** -0.5),
        "mla_kv_norm_g": 1.0 + nrm(ks[13], (N_MLA_LAYERS, KV_LORA_RANK), 0.05),
        "mla_w_kvb": nrm(ks[14], (N_MLA_LAYERS, KV_LORA_RANK, N_HEADS * (QK_NOPE_DIM + V_HEAD_DIM)), KV_LORA_RANK ** -0.5),
        "mla_w_out": nrm(ks[15], (N_MLA_LAYERS, MLA_GATE_DIM, D_MODEL), MLA_GATE_DIM ** -0.5),
    }


def reference(x_prompt, x_sample, state_conv, cache_ckv, cache_krope, norm_g, final_norm_g,
              conv_w_in, conv_w, conv_w_out, mla_w_in, mla_q_norm_g, mla_w_qb,
              mla_kv_norm_g, mla_w_kvb, mla_w_out):
    xp, xs = x_prompt, x_sample
    Bp, S, _ = xp.shape
    Bs, T, _ = xs.shape
    pos_p = jnp.arange(S, dtype=jnp.int32)
    pos_s = PAST_LEN + jnp.arange(T, dtype=jnp.int32)
    conv_p_states, conv_s_states = [], []
    ckv_p_rows, krope_p_rows, ckv_s_rows, krope_s_rows = [], [], [], []

    for i in range(DEPTH):
        j = i // N_MIXERS
        hp = rms_norm(xp, norm_g[i])
        hs = rms_norm(xs, norm_g[i])
        if i % N_MIXERS == 0:
            zeros_prev = jnp.zeros((Bp, CONV_WIDTH - 1, CONV_DIM), xp.dtype)
            yp, st_p = short_conv_mixer(hp, zeros_prev, conv_w_in[j], conv_w[j], conv_w_out[j])
            ys, st_s = short_conv_mixer(hs, state_conv[j], conv_w_in[j], conv_w[j], conv_w_out[j])
            conv_p_states.append(st_p)
            conv_s_states.append(st_s)
        else:
            qp, ckv_p, kr_p, zp = mla_project(hp, pos_p, mla_w_in[j], mla_q_norm_g[j], mla_w_qb[j], mla_kv_norm_g[j])
            kp, vp = mla_expand_kv(ckv_p, kr_p, mla_w_kvb[j])
            op = mla_prompt_attention(qp, kp, vp)
            yp = (jax.nn.silu(zp) * op) @ mla_w_out[j]

            qs, ckv_s, kr_s, zs = mla_project(hs, pos_s, mla_w_in[j], mla_q_norm_g[j], mla_w_qb[j], mla_kv_norm_g[j])
            ckv_all = jnp.concatenate([cache_ckv[j].astype(ckv_s.dtype), ckv_s], axis=1)
            kr_all = jnp.concatenate([cache_krope[j].astype(kr_s.dtype), kr_s], axis=1)
            ks_, vs_ = mla_expand_kv(ckv_all, kr_all, mla_w_kvb[j])
            os_ = attend(qs, ks_, vs_, None).reshape(Bs, T, N_HEADS * V_HEAD_DIM)
            ys = (jax.nn.silu(zs) * os_) @ mla_w_out[j]

            ckv_p_rows.append(ckv_p)
            krope_p_rows.append(kr_p)
            ckv_s_rows.append(ckv_s)
            krope_s_rows.append(kr_s)
        xp = xp + yp
        xs = xs + ys

    y_prompt = rms_norm(xp, final_norm_g)
    y_sample = rms_norm(xs, final_norm_g)
    new_conv_prompt = jnp.stack(conv_p_states, axis=0)
    new_ckv_prompt = jnp.stack(ckv_p_rows, axis=0)
    new_krope_prompt = jnp.stack(krope_p_rows, axis=0)
    new_conv_sample = jnp.stack(conv_s_states, axis=0)
    new_ckv_sample = jnp.stack(ckv_s_rows, axis=0)
    new_krope_sample = jnp.stack(krope_s_rows, axis=0)
    return (y_prompt, y_sample, new_conv_prompt, new_ckv_prompt, new_krope_prompt, new_conv_sample, new_ckv_sample, new_krope_sample)
```

```python
import math
import numpy as np
import concourse.bass as bass
import concourse.mybir as mybir
from concourse.bass_utils import run_bass_kernel_spmd

F32 = mybir.dt.float32
BF16 = mybir.dt.bfloat16
AF = mybir.ActivationFunctionType
ALU = mybir.AluOpType

D = 1024
NH = 16
EPS = 1e-6
SM_SCALE = 96 ** -0.5
MAGIC = 12582912.0
TWO_PI = 2.0 * math.pi
NEG = -30000.0


class Tok:
    __slots__ = ("w", "r")

    def __init__(self):
        self.w = []
        self.r = {}


class DSem:
    def __init__(self, h):
        self.h = h
        self.count = 0


class Prog:
    ENG = ("sp", "act", "dve", "pool", "pe")
    FENCE = ("act", "dve", "pool")

    def __init__(self):
        self.q = {e: [] for e in self.ENG}
        self.bar = {e: [] for e in self.ENG}

    def op(self, eng, fn, R=(), W=(), dsem=None):
        deps = list(self.bar[eng])
        self.bar[eng] = []
        for t in R:
            deps += t.w
        for t in W:
            deps += t.w
            deps += list(t.r.values())
        idx = len(self.q[eng])
        if dsem is not None:
            dsem.count += 16
            ref = ("dma", dsem, dsem.count)
        else:
            ref = ("op", eng, idx)
        self.q[eng].append(dict(fn=fn, deps=deps, dsem=dsem, sig=False, waits=[]))
        key = ("dma", id(dsem)) if dsem is not None else eng
        for t in R:
            t.r[key] = ref
        for t in W:
            t.w = [ref]
            t.r = {}
        return ref

    def barrier(self, extra_refs=()):
        refs = list(extra_refs)
        for e in self.ENG:
            if e == "sp":
                continue
            for i in range(len(self.q[e]) - 1, -1, -1):
                if self.q[e][i]["dsem"] is None and self.q[e][i]["fn"] is not None:
                    refs.append(("op", e, i))
                    break
        for e in self.ENG:
            self.bar[e] = list(refs)

    def finalize(self):
        for e in self.ENG:
            waited = {}
            for rec in self.q[e]:
                need = {}
                for ref in rec["deps"]:
                    if ref[0] == "op":
                        if ref[1] == e and e not in self.FENCE:
                            continue
                        k = ("op", ref[1])
                        v = ref[2]
                    else:
                        k = ("dma", ref[1])
                        v = ref[2]
                    if need.get(k, -1) < v:
                        need[k] = v
                for k, v in need.items():
                    if waited.get(k, -1) >= v:
                        continue
                    waited[k] = v
                    rec["waits"].append((k, v))
                    if k[0] == "op":
                        self.q[k[1]][v]["sig"] = True
        self.sigc = {}
        for e in self.ENG:
            c = 0
            m = {}
            for i, rec in enumerate(self.q[e]):
                if rec["sig"]:
                    c += 1
                    m[i] = c
            self.sigc[e] = m

    def check_deadlock(self):
        ptr = {e: 0 for e in self.ENG}
        semv = {e: 0 for e in self.ENG}
        dmav = {}
        progress = True
        while progress:
            progress = False
            for e in self.ENG:
                while ptr[e] < len(self.q[e]):
                    rec = self.q[e][ptr[e]]
                    ok = True
                    for k, v in rec["waits"]:
                        if k[0] == "op":
                            if semv[k[1]] < self.sigc[k[1]][v]:
                                ok = False
                        else:
                            if dmav.get(id(k[1]), 0) < v:
                                ok = False
                    if not ok:
                        break
                    if rec["dsem"] is not None:
                        dmav[id(rec["dsem"])] = dmav.get(id(rec["dsem"]), 0) + 16
                    elif rec["sig"] and rec["fn"] is not None:
                        semv[e] += 1
                    ptr[e] += 1
                    progress = True
        stuck = {e: (ptr[e], len(self.q[e])) for e in self.ENG if ptr[e] < len(self.q[e])}
        if stuck:
            msg = []
            for e, (p, n) in stuck.items():
                rec = self.q[e][p]
                msg.append((e, p, n, [(k[0], (k[1] if k[0] == "op" else "dma"), v) for k, v in rec["waits"]]))
            raise RuntimeError("deadlock in recorded program: %r" % (msg,))

    def replay(self, e, engine, sems):
        for rec in self.q[e]:
            for k, v in rec["waits"]:
                if k[0] == "op":
                    engine.wait_ge(sems[k[1]], self.sigc[k[1]][v])
                else:
                    engine.wait_ge(k[1].h, v)
            if rec["fn"] is None:
                continue
            ins = rec["fn"](engine)
            if rec["dsem"] is not None:
                ins.then_inc(rec["dsem"].h, 16)
            elif rec["sig"]:
                ins.then_inc(sems[e], 1)


def build(S=8192, PAST=4096, T=64, stop=None, WIDE=0, BCAST_DMA=True):
    NS = S // 512
    NO = NS // 2
    SO = S // 2
    nc = bass.Bass("TRN2", target_bir_lowering=False)

    def din(name, shape, dt=F32):
        return nc.dram_tensor(name, shape, dt, kind="ExternalInput").ap()

    def dout(name, shape):
        return nc.dram_tensor(name, shape, F32, kind="ExternalOutput").ap()

    xT = din("xT", [D, S])
    xhT = din("xhT", [128, 16])
    xsT = din("xsT", [D, T])
    scT = din("scT", [128, 16])
    cckvT = din("cckvT", [128, PAST])
    ckrT = din("ckrT", [32, PAST])
    posr = din("posr", [96, S + T])
    kbias = din("kbias", [1, S])
    vecs = din("vecs", [128, 64])
    sel_d = din("sel", [33, 97])
    w_in0 = din("w_in0", [D, 4096])
    w_out0 = din("w_out0", [D, D])
    w_in1 = din("w_in1", [D, 1440])
    w_qb = din("w_qb", [256, 1536])
    w_kvb = din("w_kvb", [128, 2048])
    w_out1 = din("w_out1", [D, D])

    yT = dout("yT", [D, SO])
    ysT = dout("ysT", [D, T])
    convp = dout("convp", [128, 16])
    ckvpT = dout("ckvpT", [128, S])
    krpT = dout("krpT", [32, S])
    convs = dout("convs", [128, 16])
    ckvsT = dout("ckvsT", [128, T])
    krsT = dout("krsT", [32, T])

    x1s = nc.dram_tensor("x1s", [D, SO + T], F32, kind="Internal").ap()
    x1ns = nc.dram_tensor("x1ns", [D, S + T], BF16, kind="Internal").ap()
    rrs = nc.dram_tensor("rrs", [2, 512], F32, kind="Internal").ap()

    P = Prog()
    tOUT = Tok()
    import contextlib
    es = contextlib.ExitStack()
    NW = 51200
    arena = es.enter_context(nc.sbuf_tensor("arena", [128, NW], F32))
    psb = [es.enter_context(nc.psum_tensor(f"ps{i}", [128, 512], F32)) for i in range(8)]
    pst = [Tok() for _ in range(8)]

    class Alloc:
        def __init__(self, base):
            self.p = base

        def f32(self, n, parts=(0, 128)):
            a = arena[parts[0]:parts[1], self.p:self.p + n]
            self.p += n
            assert self.p <= NW, self.p
            return a

        def bf16(self, n, parts=(0, 128)):
            w = (n + 1) // 2
            a = arena[parts[0]:parts[1], self.p:self.p + w].bitcast(BF16)
            self.p += w
            assert self.p <= NW, self.p
            return a[:, 0:n]

    def dsem(name):
        return DSem(es.enter_context(nc.semaphore(name)))

    AP_ = Alloc(0)
    VEC = AP_.f32(64)
    tVEC = Tok()
    g0, g1, gf = VEC[:, 0:8], VEC[:, 8:16], VEC[:, 16:24]
    cw = VEC[:, 24:48]
    gq = VEC[:, 48:50]
    gkv = VEC[:, 50:51]
    invf = VEC[0:96, 51:52]
    flagA = VEC[:, 52:53]
    sgn = VEC[0:96, 53:54]
    ONES = AP_.bf16(128)
    ONESF = AP_.f32(128)
    EPSB = AP_.f32(2)
    WK1 = AP_.bf16(16 * 97).rearrange("p (h c) -> p h c", h=16)
    WV = AP_.bf16(16 * 64).rearrange("p (h c) -> p h c", h=16)
    WQ = AP_.bf16(2 * 16 * 97).rearrange("p (k h c) -> p k h c", k=2, h=16)
    WQR = AP_.bf16(2 * 16 * 96).rearrange("p (k h c) -> p k h c", k=2, h=16)
    SEL = AP_.bf16(98, parts=(0, 33))[:, 0:97]
    W1R = AP_.bf16(8 * 32).rearrange("p (k c) -> p k c", k=8)
    tW = Tok()
    PBASE = AP_.p

    AR = Alloc(PBASE)
    GBASE = AR.p
    G = AR.bf16(8 * SO).rearrange("p (k n) -> p k n", k=8)
    GS = AR.bf16(8 * T).rearrange("p (k n) -> p k n", k=8)
    DBASE = AR.p
    CKVp = AR.p
    CKV = AR.bf16(S)
    KRp = AR.p
    KR = arena[0:33, KRp:KRp + S // 2].bitcast(BF16)
    COS = arena[64:96, KRp:KRp + SO // 2].bitcast(BF16)
    SIN = arena[64:96, KRp + SO // 2:KRp + S // 2].bitcast(BF16)
    AR.p += S // 2
    QN = AR.bf16(2 * SO).rearrange("p (k n) -> p k n", k=2)
    QNS = AR.bf16(2 * T).rearrange("p (k n) -> p k n", k=2)
    smp = AR.p
    COSS = arena[64:96, smp:smp + T // 2].bitcast(BF16)
    SINS = arena[64:96, smp + T // 2:smp + T].bitcast(BF16)
    KRN = arena[0:32, smp:smp + T // 2].bitcast(BF16)
    AR.p += T
    CKVN = AR.bf16(T)
    RBASE = AR.p
    tCKV = [Tok() for _ in range(max(NS, PAST // 512 + 1))]
    tKR = [Tok() for _ in range(max(NS, PAST // 512 + 1))]
    tQN = [Tok() for _ in range(NO)]
    tG = [Tok() for _ in range(NO)]
    tCS = [Tok() for _ in range(NO)]
    tSMP = Tok()
    tGS = Tok()

    sems = {}
    for e in ("act", "dve", "pool", "pe"):
        sems[e] = es.enter_context(nc.semaphore("s_" + e))

    def mm(out, lhsT, rhs, start, stop, R, W):
        return P.op("pe", lambda e: e.matmul(out, lhsT, rhs, start=start, stop=stop), R=R, W=W)

    psrr = [0]

    def ps_next():
        i = psrr[0] % 8
        psrr[0] += 1
        return i

    outrefs = []

    def dma(eng, out, in_, R, W, sem):
        isout = tOUT in W
        W = [t for t in W if t is not tOUT]
        ref = P.op(eng, lambda e: e.dma_start(out=out, in_=in_), R=R, W=W, dsem=sem)
        if isout:
            outrefs.append(ref)
        return ref

    def tt(eng, out, a, b, op, R, W):
        return P.op(eng, lambda e: e.tensor_tensor(out, a, b, op), R=R, W=W)

    def ts(eng, out, a, s1, s2, op0, op1, R, W):
        if s2 is None:
            return P.op(eng, lambda e: e.tensor_scalar(out, a, s1, None, op0), R=R, W=W)
        return P.op(eng, lambda e: e.tensor_scalar(out, a, s1, s2, op0, op1), R=R, W=W)

    def stt(eng, out, a, s, b, op0, op1, R, W):
        return P.op(eng, lambda e: e.scalar_tensor_tensor(out, a, s, b, op0, op1), R=R, W=W)

    def cp(eng, out, a, R, W):
        return P.op(eng, lambda e: e.tensor_copy(out, a), R=R, W=W)

    def act(out, a, func, R, W, scale=None):
        if scale is None:
            return P.op("act", lambda e: e.activation(out=out, in_=a, func=func), R=R, W=W)
        return P.op("act", lambda e: e.activation(out=out, in_=a, func=func, scale=scale), R=R, W=W)

    def acp(out, a, R, W):
        return P.op("act", lambda e: e.copy(out, a), R=R, W=W)

    def mset(eng, out, v, R, W):
        return P.op(eng, lambda e: e.memset(out, v), R=R, W=W)

    def rstd_from_ps(ps_i, n, inv_cnt, RSout, tRS):
        P.op("act", lambda e: e.activation(out=RSout[:, 0:n], in_=psb[ps_i][:, 0:n], func=AF.Ln, scale=inv_cnt, bias=EPSB[:, 0:1]),
             R=[pst[ps_i], tW], W=[tRS])
        P.op("act", lambda e: e.activation(out=RSout[:, 0:n], in_=RSout[:, 0:n], func=AF.Exp, scale=-0.5), R=[], W=[tRS])

    def sumsq(src3, nchunks, n, tsrc, parts=128):
        b = ps_next()
        for c in range(nchunks):
            mm(psb[b][:, 0:n], ONES[0:parts, :], src3(c), c == 0, c == nchunks - 1, R=[tsrc, tW], W=[pst[b]])
        return b

    def finish():
        P.bar["sp"] = P.bar["sp"] + outrefs
        P.op("sp", None)
        P.bar["pool"] = P.bar["pool"] + outrefs
        P.op("pool", None)
        P.finalize()
        P.check_deadlock()

        with nc.Block() as block:
            @block.sync
            def _(eng):
                P.replay("sp", eng, sems)

            @block.scalar
            def _(eng):
                P.replay("act", eng, sems)

            @block.vector
            def _(eng):
                P.replay("dve", eng, sems)

            @block.gpsimd
            def _(eng):
                P.replay("pool", eng, sems)

            @block.tensor
            def _(eng):
                P.replay("pe", eng, sems)
        es.close()
        return nc

    s_vec = dsem("d_vec")
    dma("sp", VEC, vecs[:, :], R=[], W=[tVEC], sem=s_vec)
    mset("dve", ONES, 1.0, R=[], W=[tW])
    mset("dve", ONESF, 1.0, R=[], W=[tW])
    mset("dve", EPSB, EPS, R=[], W=[tW])

    A1 = Alloc(PBASE)
    W0 = A1.bf16(8 * 4096).rearrange("p (k n) -> p k n", k=8)
    WO0 = A1.bf16(8 * 1024).rearrange("p (k n) -> p k n", k=8)
    tW0 = Tok()
    XBp = A1.p
    XB = [A1.f32(4096).rearrange("p (k n) -> p k n", k=8) for _ in range(2)]
    tXB = [Tok(), Tok()]
    sXB = [dsem("d_xb0"), dsem("d_xb1")]
    ST = [arena[:, XBp + i * 4096:XBp + (i + 1) * 4096] for i in range(2)]
    tST = [Tok(), Tok()]
    sST = [dsem("d_st0"), dsem("d_st1")]
    XSQ = A1.bf16(4096).rearrange("p (k n) -> p k n", k=8)
    tXSQ = Tok()
    XN = [A1.bf16(4096).rearrange("p (k n) -> p k n", k=8) for _ in range(2)]
    tXN = [Tok(), Tok()]
    sXN = [dsem("d_xn0"), dsem("d_xn1")]
    RS = [A1.f32(512) for _ in range(2)]
    tRS = [Tok(), Tok()]
    ZS = [A1.f32(512) for _ in range(2)]
    CS = [A1.f32(512) for _ in range(2)]
    GB = [A1.f32(512) for _ in range(2)]
    CV = [A1.f32(512) for _ in range(2)]
    U = [A1.f32(514) for _ in range(2)]
    tCH = [Tok(), Tok()]
    tZS = [Tok(), Tok()]
    tCSb = [Tok(), Tok()]
    tGB = [Tok(), Tok()]
    tCV = [Tok(), Tok()]
    tU = [Tok(), Tok()]
    HALO = A1.f32(16).rearrange("p (k n) -> p k n", k=8)
    tHALO = Tok()
    sHALO = dsem("d_halo")
    GA = [A1.bf16(4096).rearrange("p (k n) -> p k n", k=8) for _ in range(2)]
    tGA = [Tok(), Tok()]
    A1END = A1.p

    stc = [0]

    def stage_load(src_ap, view):
        i = stc[0] % 2
        stc[0] += 1
        dma("sp", view(ST[i]), src_ap, R=[], W=[tST[i]], sem=sST[i])
        return i

    engs2 = ["dve", "pool"]
    for c in range(8):
        i = stage_load(w_in0[c * 128:(c + 1) * 128, :], lambda a: a)
        for hlf in range(2):
            sl = slice(hlf * 2048, (hlf + 1) * 2048)
            ts("dve", W0[:, c, sl], ST[i][:, sl], g0[:, c:c + 1], None, ALU.mult, None, R=[tST[i], tVEC], W=[tW0])
    for q in range(2):
        i = stage_load(w_out0[q * 512:(q + 1) * 512, :].rearrange("(c p) n -> p c n", p=128),
                       lambda a: a.rearrange("p (c n) -> p c n", c=4))
        for cc in range(4):
            (acp if cc % 2 else (lambda o, a, R, W: cp("dve", o, a, R, W)))(WO0[:, q * 4 + cc, :], ST[i][:, cc * 1024:(cc + 1) * 1024], R=[tST[i]], W=[tW0])
    i = stage_load(w_in1[:, 384:416].rearrange("(c p) n -> p c n", p=128), lambda a: a[:, 0:256].rearrange("p (c n) -> p c n", c=8))
    for c in range(8):
        ts("dve", W1R[:, c, 0:16], ST[i][:, c * 32 + 16:c * 32 + 32], g1[:, c:c + 1], None, ALU.mult, None, R=[tST[i], tVEC], W=[tW])
        ts("dve", W1R[:, c, 16:32], ST[i][:, c * 32:c * 32 + 16], g1[:, c:c + 1], None, ALU.mult, None, R=[tST[i], tVEC], W=[tW])
    mset("pool", WQ.rearrange("p k h c -> p (k h c)"), 0.0, R=[], W=[tW])
    mset("pool", WQR.rearrange("p k h c -> p (k h c)"), 0.0, R=[], W=[tW])
    i = stage_load(w_qb.rearrange("(c p) n -> p c n", p=128), lambda a: a[:, 0:3072].rearrange("p (c n) -> p c n", c=2))
    for c in range(2):
        src = ST[i][:, c * 1536:(c + 1) * 1536].rearrange("p (h d) -> p h d", h=16)
        cp("dve", WQ[:, c, :, 0:96], src, R=[tST[i]], W=[tW])
        cp("pool", WQR[:, c, :, 64:80], src[:, :, 80:96], R=[tST[i]], W=[tW])
        cp("pool", WQR[:, c, :, 80:96], src[:, :, 64:80], R=[tST[i]], W=[tW])
    mset("pool", WK1.rearrange("p h c -> p (h c)"), 0.0, R=[], W=[tW])
    i = stage_load(w_kvb[:, :], lambda a: a[:, 0:2048])
    src = ST[i][:, 0:2048].rearrange("p (h d) -> p h d", h=16)
    cp("dve", WK1[:, :, 0:64], src[:, :, 0:64], R=[tST[i]], W=[tW])
    cp("pool", WV[:, :, :], src[:, :, 64:128], R=[tST[i]], W=[tW])
    i = stage_load(sel_d[:, :], lambda a: a[0:33, 0:97])
    cp("dve", SEL, ST[i][0:33, 0:97], R=[tST[i]], W=[tW])
    P.barrier()

    if stop == "W":
        return finish()
    def s1_load(si, src, n):
        b = si % 2
        srcv = src.rearrange("p (c k) -> p c k", k=2) if src is xhT else src.rearrange("(c p) n -> p c n", p=128)
        dma("sp", XB[b][:, :, 0:n], srcv, R=[], W=[tXB[b]], sem=sXB[b])

    def s1_square(si, n):
        b = si % 2
        m = max(n, WIDE)
        act(XSQ[:, :, 0:m], XB[b][:, :, 0:m], AF.Square, R=[tXB[b]], W=[tXSQ])

    def s1_stats(si, n):
        b = si % 2
        m = max(n, WIDE)
        pb = sumsq(lambda c: XSQ[:, c, 0:m], 8, m, tXSQ)
        rstd_from_ps(pb, m, 1.0 / D, RS[b], tRS[b])

    def s1_norm(si, n, eng="dve"):
        b = si % 2
        m = max(n, WIDE)
        tt(eng, XN[b][:, :, 0:m], XB[b][:, :, 0:m], RS[b][:, 0:m].unsqueeze(1).broadcast_to([128, 8, m]), ALU.mult,
           R=[tXB[b], tRS[b]], W=[tXN[b]])

    def a1_stage1(si, src, n):
        s1_load(si, src, n)
        s1_square(si, n)
        s1_stats(si, n)
        s1_norm(si, n, "dve")

    chc = [0]

    def a1_chunk(si, j, n, full=True):
        b = si % 2
        m = max(n, WIDE)
        cb = chc[0] % 2
        chc[0] += 1
        banks = {}
        roles = (("c", 1024), ("xv", 2048), ("b", 0), ("z", 3072)) if full else (("c", 1024), ("xv", 2048))
        for role, cb0 in roles:
            pb = ps_next()
            banks[role] = pb
            col = cb0 + j * 128
            for k in range(8):
                mm(psb[pb][:, 0:m], W0[:, k, col:col + 128], XN[b][:, k, 0:m], k == 0, k == 7, R=[tW0, tXN[b]], W=[pst[pb]])
        P.op("act", lambda e, o=CS[cb][:, 0:m], a=psb[banks["c"]][:, 0:m]: e.copy(o, a), R=[pst[banks["c"]]], W=[tCSb[cb]])
        cp("dve", U[cb][:, 0:2], HALO[:, j, :], R=[tHALO], W=[tU[cb]])
        tt("dve", U[cb][:, 2:2 + m], psb[banks["xv"]][:, 0:m], CS[cb][:, 0:m], ALU.mult, R=[pst[banks["xv"]], tCSb[cb]], W=[tU[cb]])
        cp("dve", HALO[:, j, :], U[cb][:, n:n + 2], R=[tU[cb]], W=[tHALO])
        if not full:
            return
        act(ZS[cb][:, 0:m], psb[banks["z"]][:, 0:m], AF.Silu, R=[pst[banks["z"]]], W=[tZS[cb]])
        tt("dve", GB[cb][:, 0:m], psb[banks["b"]][:, 0:m], ZS[cb][:, 0:m], ALU.mult, R=[pst[banks["b"]], tZS[cb]], W=[tGB[cb]])
        ts("dve", CV[cb][:, 0:m], U[cb][:, 0:m], cw[:, 3 * j:3 * j + 1], None, ALU.mult, None, R=[tU[cb], tVEC], W=[tCV[cb]])
        stt("dve", CV[cb][:, 0:m], U[cb][:, 1:1 + m], cw[:, 3 * j + 1:3 * j + 2], CV[cb][:, 0:m], ALU.mult, ALU.add, R=[tU[cb]], W=[tCV[cb]])
        stt("dve", CV[cb][:, 0:m], U[cb][:, 2:2 + m], cw[:, 3 * j + 2:3 * j + 3], CV[cb][:, 0:m], ALU.mult, ALU.add, R=[tU[cb]], W=[tCV[cb]])
        tt("dve", GA[b][:, j, 0:m], GB[cb][:, 0:m], CV[cb][:, 0:m], ALU.mult, R=[tGB[cb], tCV[cb]], W=[tGA[b]])

    def s3a(si, n):
        b = si % 2
        m = max(n, WIDE)
        for dch in range(8):
            pb = ps_next()
            for k in range(8):
                mm(psb[pb][:, 0:m], WO0[:, k, dch * 128:(dch + 1) * 128], GA[b][:, k, 0:m], k == 0, k == 7, R=[tW0, tGA[b]], W=[pst[pb]])
            tt("dve", XB[b][:, dch, 0:m], psb[pb][:, 0:m], XB[b][:, dch, 0:m], ALU.add, R=[pst[pb]], W=[tXB[b]])
        act(XSQ[:, :, 0:m], XB[b][:, :, 0:m], AF.Square, R=[tXB[b]], W=[tXSQ])

    def s3b(si, n, x1n_dst, x1_dst):
        b = si % 2
        m = max(n, WIDE)
        pb = sumsq(lambda c: XSQ[:, c, 0:m], 8, m, tXSQ)
        rstd_from_ps(pb, m, 1.0 / D, RS[b], tRS[b])
        tt("dve", XN[b][:, :, 0:m], XB[b][:, :, 0:m], RS[b][:, 0:m].unsqueeze(1).broadcast_to([128, 8, m]), ALU.mult,
           R=[tXB[b], tRS[b]], W=[tXN[b]])
        dma("sp", x1n_dst.rearrange("(c p) n -> p c n", p=128), XN[b][:, :, 0:n], R=[tXN[b]], W=[tX1NS], sem=sXN[b])
        if x1_dst is not None:
            dma("sp", x1_dst.rearrange("(c p) n -> p c n", p=128), XB[b][:, :, 0:n], R=[tXB[b]], W=[tX1S], sem=sXB[b])

    tX1NS = Tok()
    tX1S = Tok()
    print("sbuf words: A1 end", A1END, "RBASE", RBASE, "remaining bytes", nc.sbuf_bytes_remaining)
    slots = []
    for s in range(NS):
        slots.append(dict(src=xT[:, s * 512:(s + 1) * 512], n=512, own=(s // 2 if s % 2 == 0 else None),
                          x1n=x1ns[:, s * 512:(s + 1) * 512],
                          x1=(x1s[:, (s // 2) * 512:(s // 2 + 1) * 512] if s % 2 == 0 else None)))
    slots.append(dict(src=xsT[:, :], n=T, own=NO, x1n=x1ns[:, S:S + T], x1=x1s[:, SO:SO + T]))

    mset("pool", HALO.rearrange("p k n -> p (k n)"), 0.0, R=[], W=[tHALO])
    a1_stage1(1, xhT, 2)
    for j in range(8):
        a1_chunk(1, j, 2, full=False)
    if stop == "A1a":
        return finish()
    a1_stage1(0, slots[0]["src"], slots[0]["n"])
    s1_load(1, slots[1]["src"], slots[1]["n"])
    L_ = len(slots)

    def pre_slot(si):
        if si == NS - 1:
            ts("dve", HALO.rearrange("p k n -> p (k n)"), HALO.rearrange("p k n -> p (k n)"), flagA, None, ALU.mult, None,
               R=[tVEC], W=[tHALO])
        if si == NS:
            dma("sp", convp.rearrange("p (c k) -> p c k", k=2), HALO, R=[tHALO], W=[tOUT], sem=sHALO)
            dma("sp", HALO, scT.rearrange("p (c k) -> p c k", k=2), R=[], W=[tHALO], sem=sHALO)

    pre_slot(0)
    a1_chunk(0, 0, slots[0]["n"])
    for si, sd in enumerate(slots):
        n = sd["n"]
        nx = slots[si + 1] if si + 1 < L_ else None
        for j in range(1, 8):
            a1_chunk(si, j, n)
            if j == 1 and si >= 1:
                pv = slots[si - 1]
                s3b(si - 1, pv["n"], pv["x1n"], pv["x1"])
                if nx is not None:
                    s1_load(si + 1, nx["src"], nx["n"])
            if nx is not None:
                if j == 3:
                    s1_square(si + 1, nx["n"])
                if j == 5:
                    s1_stats(si + 1, nx["n"])
                if j == 6:
                    s1_norm(si + 1, nx["n"])
        if nx is not None:
            pre_slot(si + 1)
            a1_chunk(si + 1, 0, nx["n"])
        s3a(si, n)
        if si == L_ - 1:
            s3b(si, n, sd["x1n"], sd["x1"])
    dma("sp", convs.rearrange("p (c k) -> p c k", k=2), HALO, R=[tHALO], W=[tOUT], sem=sHALO)
    P.barrier([("dma", sXB[0], sXB[0].count), ("dma", sXB[1], sXB[1].count), ("dma", sXN[0], sXN[0].count),
               ("dma", sXN[1], sXN[1].count), ("dma", sHALO, sHALO.count)])

    if stop == "A1":
        return finish()
    A2 = Alloc(RBASE)
    A2g = Alloc(GBASE) if DBASE - GBASE >= 12700 else A2
    W1 = A2g.bf16(8 * 416).rearrange("p (k n) -> p k n", k=8)
    tW1 = Tok()
    ST2 = [A2g.f32(2880) for _ in range(2)]
    tST2 = [Tok(), Tok()]
    sST2 = [dsem("d_st2a"), dsem("d_st2b")]
    X1N = [A2g.bf16(4096).rearrange("p (k n) -> p k n", k=8) for _ in range(2)]
    tX1N = [Tok(), Tok()]
    sX1N = [dsem("d_x1n0"), dsem("d_x1n1")]
    POS = [A2g.f32(512, parts=(0, 96)) for _ in range(2)]
    assert A2g is A2 or A2g.p <= DBASE
    tPOS = [Tok(), Tok()]
    sPOS = [dsem("d_pos0"), dsem("d_pos1")]
    KBFb = [A2.f32(min(2048, S), parts=(32, 33)) for _ in range(2)]
    tKBFb = [Tok(), Tok()]
    sKBFb = [dsem("d_kbf0"), dsem("d_kbf1")]
    ANG = A2.f32(512, parts=(0, 96))
    T1 = A2.f32(512, parts=(0, 96))
    RSN = A2.f32(512, parts=(0, 96))
    RC = A2.f32(512, parts=(0, 96))
    SINTb = [A2.f32(512, parts=(0, 96)) for _ in range(2)]
    COSTb = [A2.f32(512, parts=(0, 96)) for _ in range(2)]
    tTR = Tok()
    tSCb = [Tok(), Tok()]
    CKVF = [A2.f32(512) for _ in range(2)]
    tCKVF = [Tok(), Tok()]
    sCKVF = [dsem("d_ckvf0"), dsem("d_ckvf1")]
    KRF = [A2.f32(512, parts=(0, 32)) for _ in range(2)]
    tKRF = [Tok(), Tok()]
    sKRF = [dsem("d_krf0"), dsem("d_krf1")]
    KTMP = A2.f32(512, parts=(0, 32))
    tKTMP = Tok()
    KVSQ = A2.bf16(512)
    tKVSQ = Tok()
    RSK = A2.f32(512)
    tRSK = Tok()
    QLF = A2.f32(1024).rearrange("p (k n) -> p k n", k=2)
    tQLF = Tok()
    QSQ = A2.bf16(1024).rearrange("p (k n) -> p k n", k=2)
    tQSQ = Tok()
    RSQ = A2.f32(512)
    tRSQ = Tok()

    for q in range(4):
        i = q % 2
        dma("sp", ST2[i].rearrange("p (c n) -> p c n", c=2), w_in1[q * 256:(q + 1) * 256, :].rearrange("(c p) n -> p c n", p=128),
            R=[], W=[tST2[i]], sem=sST2[i])
        for cc in range(2):
            c = q * 2 + cc
            ts("dve", W1[:, c, :], ST2[i][:, cc * 1440:cc * 1440 + 416], g1[:, c:c + 1], None, ALU.mult, None,
               R=[tST2[i], tVEC], W=[tW1])

    def a2_load(si, sd):
        n = sd["n"]
        b = si % 2
        is_s = si == NS
        pc = S if is_s else si * 512
        dma("sp", X1N[b][:, :, 0:n], sd["x1n"].rearrange("(c p) n -> p c n", p=128), R=[tX1NS], W=[tX1N[b]], sem=sX1N[b])
        dma("sp", POS[b][:, 0:n], posr[:, pc:pc + n], R=[], W=[tPOS[b]], sem=sPOS[b])

    def a2_trig(si, sd):
        n = sd["n"]
        m = max(n, WIDE)
        b = si % 2
        own = sd["own"]
        is_s = si == NS
        ts("dve", ANG[:, 0:m], POS[b][:, 0:m], invf, None, ALU.mult, None, R=[tPOS[b], tVEC], W=[tTR])
        ts("dve", T1[:, 0:m], ANG[:, 0:m], 1.0 / TWO_PI, MAGIC, ALU.mult, ALU.add, R=[], W=[tTR])
        ts("dve", T1[:, 0:m], T1[:, 0:m], MAGIC, -TWO_PI, ALU.subtract, ALU.mult, R=[], W=[tTR])
        tt("dve", RSN[:, 0:m], T1[:, 0:m], ANG[:, 0:m], ALU.add, R=[tSCb[b]], W=[tTR])
        ts("dve", RC[:, 0:m], RSN[:, 0:m], math.pi / 2, None, ALU.add, None, R=[], W=[tTR])
        ts("dve", T1[:, 0:m], RC[:, 0:m], math.pi, TWO_PI, ALU.is_gt, ALU.mult, R=[], W=[tTR])
        tt("dve", RC[:, 0:m], RC[:, 0:m], T1[:, 0:m], ALU.subtract, R=[], W=[tTR])
        for XX in (RSN, RC):
            ts("dve", XX[:, 0:m], XX[:, 0:m], -3.14159, None, ALU.max, None, R=[], W=[tTR])
            ts("dve", XX[:, 0:m], XX[:, 0:m], 3.14159, None, ALU.min, None, R=[], W=[tTR])
        ts("dve", RSN[:, 0:m], RSN[:, 0:m], sgn, None, ALU.mult, None, R=[tVEC], W=[tTR])
        act(SINTb[b][:, 0:m], RSN[:, 0:m], AF.Sin, R=[tTR], W=[tSCb[b]])
        act(COSTb[b][:, 0:m], RC[:, 0:m], AF.Sin, R=[tTR], W=[tSCb[b]])
        if own is not None:
            if is_s:
                acp(COSS[:, 0:n], COSTb[b][64:96, 0:n], R=[tSCb[b]], W=[tSMP])
                acp(SINS[:, 0:n], SINTb[b][64:96, 0:n], R=[tSCb[b]], W=[tSMP])
            else:
                acp(COS[:, own * 512:own * 512 + n], COSTb[b][64:96, 0:n], R=[tSCb[b]], W=[tCS[own]])
                acp(SIN[:, own * 512:own * 512 + n], SINTb[b][64:96, 0:n], R=[tSCb[b]], W=[tCS[own]])

    a2_deferred = []

    def a2_slot(si, sd):
        n = sd["n"]
        m = max(n, WIDE)
        b = si % 2
        own = sd["own"]
        is_s = si == NS
        pc = S if is_s else si * 512
        SINT, COST, tSC = SINTb[b], COSTb[b], tSCb[b]
        pkv = ps_next()
        for k in range(8):
            mm(psb[pkv][:, 0:m], W1[:, k, 256:384], X1N[b][:, k, 0:m], k == 0, k == 7, R=[tW1, tX1N[b]], W=[pst[pkv]])
        pkr = ps_next()
        for k in range(8):
            mm(psb[pkr][0:32, 0:m], W1[:, k, 384:416], X1N[b][:, k, 0:m], k == 0, k == 7, R=[tW1, tX1N[b]], W=[pst[pkr]])
        pkq = ps_next()
        for k in range(8):
            mm(psb[pkq][0:32, 0:m], W1R[:, k, :], X1N[b][:, k, 0:m], k == 0, k == 7, R=[tW, tX1N[b]], W=[pst[pkq]])
        cp("dve", CKVF[b][:, 0:m], psb[pkv][:, 0:m], R=[pst[pkv]], W=[tCKVF[b]])
        act(KVSQ[:, 0:m], psb[pkv][:, 0:m], AF.Square, R=[pst[pkv]], W=[tKVSQ])
        pss = sumsq(lambda c: KVSQ[:, 0:m], 1, m, tKVSQ)
        rstd_from_ps(pss, m, 1.0 / 128, RSK, tRSK)
        stt("dve", CKVF[b][:, 0:m], CKVF[b][:, 0:m], gkv, RSK[:, 0:m], ALU.mult, ALU.mult, R=[tRSK, tVEC], W=[tCKVF[b]])
        if is_s:
            acp(CKVN[:, 0:n], CKVF[b][:, 0:n], R=[tCKVF[b]], W=[tSMP])
            a2_deferred.append(lambda b=b, n=n: dma("sp", ckvsT[:, 0:n], CKVF[b][:, 0:n], R=[tCKVF[b]], W=[tOUT], sem=sCKVF[b]))
        else:
            acp(CKV[:, pc:pc + n], CKVF[b][:, 0:n], R=[tCKVF[b]], W=[tCKV[si]])
            a2_deferred.append(lambda b=b, n=n, pc=pc: dma("sp", ckvpT[:, pc:pc + n], CKVF[b][:, 0:n], R=[tCKVF[b]], W=[tOUT], sem=sCKVF[b]))
        tt("dve", KRF[b][:, 0:m], psb[pkr][0:32, 0:m], COST[0:32, 0:m], ALU.mult, R=[pst[pkr], tSC], W=[tKRF[b]])
        tt("dve", KTMP[:, 0:m], psb[pkq][0:32, 0:m], SINT[0:32, 0:m], ALU.mult, R=[pst[pkq], tSC], W=[tKTMP])
        tt("dve", KRF[b][:, 0:m], KRF[b][:, 0:m], KTMP[:, 0:m], ALU.add, R=[tKTMP], W=[tKRF[b]])
        if is_s:
            acp(KRN[:, 0:n], KRF[b][:, 0:n], R=[tKRF[b]], W=[tSMP])
            a2_deferred.append(lambda b=b, n=n: dma("sp", krsT[:, 0:n], KRF[b][:, 0:n], R=[tKRF[b]], W=[tOUT], sem=sKRF[b]))
        else:
            acp(KR[0:32, pc:pc + n], KRF[b][:, 0:n], R=[tKRF[b]], W=[tKR[si]])
            a2_deferred.append(lambda b=b, n=n, pc=pc: dma("sp", krpT[:, pc:pc + n], KRF[b][:, 0:n], R=[tKRF[b]], W=[tOUT], sem=sKRF[b]))
        if own is not None:
            pq = []
            for c in range(2):
                pb = ps_next()
                pq.append(pb)
                for k in range(8):
                    mm(psb[pb][:, 0:m], W1[:, k, c * 128:(c + 1) * 128], X1N[b][:, k, 0:m], k == 0, k == 7, R=[tW1, tX1N[b]], W=[pst[pb]])
                cp("dve", QLF[:, c, 0:m], psb[pb][:, 0:m], R=[pst[pb]], W=[tQLF])
            act(QSQ[:, :, 0:m], QLF[:, :, 0:m], AF.Square, R=[tQLF], W=[tQSQ])
            pss = sumsq(lambda c: QSQ[:, c, 0:m], 2, m, tQSQ)
            rstd_from_ps(pss, m, 1.0 / 256, RSQ, tRSQ)
            for c in range(2):
                if is_s:
                    dst, tk = QNS[:, c, 0:n], tSMP
                else:
                    dst, tk = QN[:, c, own * 512:own * 512 + n], tQN[own]
                stt("dve", dst, QLF[:, c, 0:n], gq[:, c:c + 1], RSQ[:, 0:n], ALU.mult, ALU.mult, R=[tQLF, tRSQ, tVEC], W=[tk])

    a2_load(0, slots[0])
    a2_trig(0, slots[0])
    KW = min(2048, S)
    for g in range(S // KW):
        kb_ = KBFb[g % 2]
        dma("sp", kb_[:, 0:KW], kbias[:, g * KW:(g + 1) * KW], R=[], W=[tKBFb[g % 2]], sem=sKBFb[g % 2])
        cp("dve", KR[32:33, g * KW:(g + 1) * KW], kb_[:, 0:KW], R=[tKBFb[g % 2]], W=[tKR[i_] for i_ in range(g * KW // 512, (g + 1) * KW // 512)])
    for si, sd in enumerate(slots):
        if si + 1 < len(slots):
            a2_load(si + 1, slots[si + 1])
        while a2_deferred:
            a2_deferred.pop(0)()
        a2_slot(si, sd)
        if si + 1 < len(slots):
            a2_trig(si + 1, slots[si + 1])
    while a2_deferred:
        a2_deferred.pop(0)()
    P.barrier([("dma", sCKVF[0], sCKVF[0].count), ("dma", sCKVF[1], sCKVF[1].count),
               ("dma", sKRF[0], sKRF[0].count), ("dma", sKRF[1], sKRF[1].count)])

    if stop == "A2":
        return finish()
    AB = Alloc(RBASE)
    KT = AB.bf16(S, parts=(0, 97))
    V = AB.bf16(NS * 4 * 128).rearrange("p (t c) -> p t c", c=128)
    QT = AB.bf16(SO, parts=(0, 97))
    PT = [AB.bf16(512) for _ in range(4)]
    tPT = [Tok() for _ in range(4)]
    RR = AB.f32(512)
    tRR = Tok()
    OT = AB.f32(512)
    tOT = Tok()
    RRb = [RR, AB.f32(512)]
    OTb = [OT, AB.f32(512)]
    tRRb = [tRR, Tok()]
    tOTb = [tOT, Tok()]
    RBb = [AB.f32(512), AB.f32(512)]
    tRBb = [Tok(), Tok()]
    sRBb = [dsem("d_rb0"), dsem("d_rb1")]
    tRRS = [Tok(), Tok()]
    R1 = AB.f32(512, parts=(64, 96))
    R2 = AB.f32(512, parts=(64, 96))
    tR12 = Tok()
    STG = [AB.f32(512) for _ in range(2)]
    tSTG = [Tok(), Tok()]
    sSTG = [dsem("d_stg0"), dsem("d_stg1")]
    tKT = [Tok() for _ in range(max(NS, PAST // 512 + 1))]
    tV = Tok()
    tQT = Tok()
    mset("pool", QT[96:97, :], 1.0, R=[], W=[tQT])

    exr = [0]

    def ex_bank():
        i = 6 + exr[0] % 2
        exr[0] += 1
        return i

    def rope_rows(kgroups):
        for (c0, n, ti) in kgroups:
            pb = ex_bank()
            mm(psb[pb][0:97, 0:n], SEL[0:33, :], KR[0:33, c0:c0 + n], True, True, R=[tW, tKR[ti]], W=[pst[pb]])
            cp("dve", KT[64:97, c0:c0 + n], psb[pb][64:97, 0:n], R=[pst[pb]], W=[tKT[ti]])

    uc = [0]
    oc = [0]

    tVs = [Tok() for _ in range(max(NS, PAST // 512 + 1))]
    tQTi = [Tok() for _ in range(NO)]

    def ktask(h, sl):
        pb = ex_bank()
        c0 = sl * 512
        mm(psb[pb][0:64, 0:512], WK1[:, h, 0:64], CKV[:, c0:c0 + 512], True, True, R=[tW, tCKV[sl]], W=[pst[pb]])
        cp("dve", KT[0:64, c0:c0 + 512], psb[pb][0:64, 0:512], R=[pst[pb]], W=[tKT[sl]])

    def vtask(h, sl):
        par = h % 2
        voff = 64 * par
        onec = 64 if par == 0 else 0
        pb = ex_bank()
        for ii in range(4):
            c0 = sl * 512 + ii * 128
            mm(psb[pb][:, ii * 64:(ii + 1) * 64], CKV[:, c0:c0 + 128], WV[:, h, :], True, True, R=[tW, tCKV[sl]], W=[pst[pb]])
        cp("dve", V[:, sl * 4:sl * 4 + 4, voff:voff + 64], psb[pb][:, 0:256].rearrange("p (t c) -> p t c", c=64), R=[pst[pb]], W=[tVs[sl]])
        mset("pool", V[:, sl * 4:sl * 4 + 4, onec:onec + 1], 1.0, R=[], W=[tVs[sl]])

    def qtask(h, i):
        c0 = i * 512
        pa = ex_bank()
        pb2 = ex_bank()
        for c in range(2):
            mm(psb[pa][0:96, 0:512], WQ[:, c, h, 0:96], QN[:, c, c0:c0 + 512], c == 0, c == 1, R=[tW, tQN[i]], W=[pst[pa]])
        for c in range(2):
            mm(psb[pb2][0:96, 0:512], WQR[:, c, h, :], QN[:, c, c0:c0 + 512], c == 0, c == 1, R=[tW, tQN[i]], W=[pst[pb2]])
        cp("dve", QT[0:64, c0:c0 + 512], psb[pa][0:64, 0:512], R=[pst[pa]], W=[tQTi[i]])
        tt("dve", R1[:, 0:512], psb[pa][64:96, 0:512], COS[:, c0:c0 + 512], ALU.mult, R=[pst[pa], tCS[i]], W=[tR12])
        tt("dve", R2[:, 0:512], psb[pb2][64:96, 0:512], SIN[:, c0:c0 + 512], ALU.mult, R=[pst[pb2], tCS[i]], W=[tR12])
        tt("pool", QT[64:96, c0:c0 + 512], R1[:, 0:512], R2[:, 0:512], ALU.add, R=[tR12], W=[tQTi[i]])

    def prompt_tiles(h):
        tiles = []
        for i in range(NO - 1, -1, -1):
            units = []
            for sl in list(range(2 * i - 1, -1, -1)) + [NS - 1]:
                for u in range(4):
                    units.append((sl * 512 + u * 128, 128, sl * 4 + u, 0, False, sl))
            for u in range(4):
                units.append((2 * i * 512 + u * 128, 128, 2 * i * 4 + u, 128 * u, True, 2 * i))
            freed = [2 * i, 2 * i - 1] if i >= 1 else [0, NS - 1]
            tiles.append(dict(qc0=i * 512, nq=512, units=units, ti=i, freed=freed, qtok=tQTi[i],
                              gdst=(lambda vr, i=i, h=h: G[vr, h // 2, i * 512:(i + 1) * 512]), gtok=tG[i]))
        return tiles

    def attend_prompt(h, tiles, next_h, extra_tasks=None):
        par = h % 2
        vr = slice(0, 64) if par == 0 else slice(64, 128)
        sr = 64 if par == 0 else 0
        mcols = 65 if par == 0 else 128
        flat = []
        for tl in tiles:
            ob = 4 + oc[0] % 2
            oc[0] += 1
            nu = len(tl["units"])
            for ui, u in enumerate(tl["units"]):
                flat.append((tl, u, ui == 0, ui == nu - 1, ob, ui))
        pend = g_pend
        tasks = list(extra_tasks) if extra_tasks else []
        LA = 3
        idx = 0
        while idx < len(flat) + LA or tasks:
            if idx < len(flat):
                tl, u, first, last, ob, ui_ = flat[idx]
                kc0, nk, vt, qlo, diag, ktok = u
                nq = tl["nq"] - qlo
                sb = uc[0] % 4
                uc[0] += 1
                flat[idx] = flat[idx] + (sb,)
                sub = tl.get("sub", {}).get(ui_, 1)
                for u_ in range(sub):
                    mm(psb[sb][0:nk, u_ * nq:(u_ + 1) * nq], KT[0:97, kc0 + u_ * 128:kc0 + u_ * 128 + nk],
                       QT[0:97, tl["qc0"] + qlo:tl["qc0"] + tl["nq"]], True, True,
                       R=[tKT[ktok], tl["qtok"], tQT], W=[pst[sb]])
                act(PT[sb][0:nk, 0:sub * nq], psb[sb][0:nk, 0:sub * nq], AF.Exp, R=[pst[sb]], W=[tPT[sb]], scale=SM_SCALE)
                if diag:
                    mset("pool", PT[sb][64:128, 0:64], 0.0, R=[], W=[tPT[sb]])
            j = idx - LA
            if 0 <= j < len(flat):
                tl, u, first, last, ob, ui_, sb = flat[j]
                kc0, nk, vt, qlo, diag, ktok = u
                nq = tl["nq"] - qlo
                sub = tl.get("sub", {}).get(ui_, 1)
                for u_ in range(sub):
                    mm(psb[ob][0:mcols, qlo:tl["nq"]], V[0:nk, vt + u_, 0:mcols], PT[sb][0:nk, u_ * nq:(u_ + 1) * nq],
                       first and u_ == 0, last and u_ == sub - 1, R=[tVs[ktok], tPT[sb]], W=[pst[ob]])
                for fn_ in tl.get("posts", {}).get(ui_, []):
                    tasks.append(fn_)
                if last:
                    nqt = tl["nq"]
                    act(RRb[ob - 4][sr:sr + 1, 0:nqt], psb[ob][sr:sr + 1, 0:nqt], AF.Ln, R=[pst[ob]], W=[tRRb[ob - 4]])
                    act(RRb[ob - 4][sr:sr + 1, 0:nqt], RRb[ob - 4][sr:sr + 1, 0:nqt], AF.Exp, R=[], W=[tRRb[ob - 4]], scale=-1.0)
                    cp("dve", OTb[ob - 4][vr, 0:nqt], psb[ob][vr, 0:nqt], R=[pst[ob]], W=[tOTb[ob - 4]])
                    if BCAST_DMA:
                        k_ = ob - 4
                        dma("sp", rrs[k_:k_ + 1, 0:nqt], RRb[k_][sr:sr + 1, 0:nqt], R=[tRRb[k_]], W=[tRRS[k_]], sem=sRBb[k_])
                        dma("sp", RBb[k_][vr, 0:nqt], rrs[k_:k_ + 1, 0:nqt].partition_broadcast(64),
                            R=[tRRS[k_]], W=[tRBb[k_]], sem=sRBb[k_])
                    pend.append((g_cnt[0] + (14 if BCAST_DMA else 9), tl, ob, nqt, vr))
                    if next_h is not None and "freed" in tl:
                        for sl in tl["freed"]:
                            tasks.append(lambda sl=sl: ktask(next_h, sl))
                            tasks.append(lambda sl=sl: vtask(next_h, sl))
                        tasks.append(lambda i=tl["ti"]: qtask(next_h, i))
            if tasks and (idx % 2 == 0 or idx >= len(flat)):
                tasks.pop(0)()
            g_cnt[0] += 1
            flush_pend(g_cnt[0])
            idx += 1

    g_pend = []
    g_cnt = [0]

    def flush_pend(upto):
        if True:
            while g_pend and g_pend[0][0] <= upto:
                _, tl, ob, nqt, vr = g_pend.pop(0)
                sr = 64 if vr.start == 0 else 0
                if BCAST_DMA:
                    tt("dve", tl["gdst"](vr), OTb[ob - 4][vr, 0:nqt], RBb[ob - 4][vr, 0:nqt], ALU.mult,
                       R=[tOTb[ob - 4], tRBb[ob - 4]], W=[tl["gtok"]])
                else:
                    bb = ex_bank()
                    mm(psb[bb][:, 0:nqt], ONESF[sr:sr + 1, :], RRb[ob - 4][sr:sr + 1, 0:nqt], True, True, R=[tW, tRRb[ob - 4]], W=[pst[bb]])
                    tt("dve", tl["gdst"](vr), OTb[ob - 4][vr, 0:nqt], psb[bb][vr, 0:nqt], ALU.mult, R=[tOTb[ob - 4], pst[bb]], W=[tl["gtok"]])

    rope_rows([(s_ * 512, 512, s_) for s_ in range(NS)])
    for sl in range(NS):
        ktask(0, sl)
        vtask(0, sl)
    for i in range(NO):
        qtask(0, i)
    NKG = PAST // 512
    cache_tasks = []
    for g in range(NKG):
        def _t(g=g):
            i = g % 2
            dma("sp", STG[i], cckvT[:, g * 512:(g + 1) * 512], R=[], W=[tSTG[i]], sem=sSTG[i])
            cp("dve", CKV[:, g * 512:(g + 1) * 512], STG[i], R=[tSTG[i]], W=[tCKV[g]])
        cache_tasks.append(_t)
    for g in range(NKG):
        def _t(g=g):
            i = g % 2
            dma("sp", STG[i][0:32, :], ckrT[:, g * 512:(g + 1) * 512], R=[], W=[tSTG[i]], sem=sSTG[i])
            cp("dve", KR[0:32, g * 512:(g + 1) * 512], STG[i][0:32, :], R=[tSTG[i]], W=[tKR[g]])
        cache_tasks.append(_t)
    for h in range(NH):
        attend_prompt(h, prompt_tiles(h), h + 1 if h + 1 < NH else None, extra_tasks=cache_tasks if h == NH - 1 else None)
    flush_pend(1 << 60)
    P.barrier()

    if stop == "B":
        return finish()
    mset("pool", KR[32:33, 0:PAST + T], 0.0, R=[], W=[tKR[g] for g in range(NKG + 1)])
    cp("pool", CKV[:, PAST:PAST + T], CKVN[:, 0:T], R=[tSMP], W=[tCKV[NKG]])
    cp("pool", KR[0:32, PAST:PAST + T], KRN[:, 0:T], R=[tSMP], W=[tKR[NKG]])
    kgroups_s = [(g * 512, 512, g) for g in range(NKG)] + [(PAST, T, NKG)]
    tQTs = [Tok(), Tok()]

    def ktask_s(h, g):
        c0, n, _ = kgroups_s[g]
        pb = ex_bank()
        mm(psb[pb][0:64, 0:n], WK1[:, h, 0:64], CKV[:, c0:c0 + n], True, True, R=[tW, tCKV[g]], W=[pst[pb]])
        cp("dve", KT[0:64, c0:c0 + n], psb[pb][0:64, 0:n], R=[pst[pb]], W=[tKT[g]])

    def vtask_s(h, g):
        par = h % 2
        voff = 64 * par
        onec = 64 if par == 0 else 0
        pb = ex_bank()
        if g < NKG:
            for ii in range(4):
                c0 = g * 512 + ii * 128
                mm(psb[pb][:, ii * 64:(ii + 1) * 64], CKV[:, c0:c0 + 128], WV[:, h, :], True, True, R=[tW, tCKV[g]], W=[pst[pb]])
            cp("dve", V[:, g * 4:g * 4 + 4, voff:voff + 64], psb[pb][:, 0:256].rearrange("p (t c) -> p t c", c=64), R=[pst[pb]], W=[tVs[g]])
            mset("pool", V[:, g * 4:g * 4 + 4, onec:onec + 1], 1.0, R=[], W=[tVs[g]])
        else:
            vt = PAST // 128
            mm(psb[pb][0:T, 0:64], CKV[:, PAST:PAST + T], WV[:, h, :], True, True, R=[tW, tCKV[g]], W=[pst[pb]])
            cp("dve", V[0:T, vt, voff:voff + 64], psb[pb][0:T, 0:64], R=[pst[pb]], W=[tVs[g]])
            mset("pool", V[0:T, vt:vt + 1, onec:onec + 1], 1.0, R=[], W=[tVs[g]])

    def qtask_s(h):
        c0 = (h % 2) * 64
        pa = ex_bank()
        pb2 = ex_bank()
        for c in range(2):
            mm(psb[pa][0:96, 0:T], WQ[:, c, h, 0:96], QNS[:, c, 0:T], c == 0, c == 1, R=[tW, tSMP], W=[pst[pa]])
        for c in range(2):
            mm(psb[pb2][0:96, 0:T], WQR[:, c, h, :], QNS[:, c, 0:T], c == 0, c == 1, R=[tW, tSMP], W=[pst[pb2]])
        cp("dve", QT[0:64, c0:c0 + T], psb[pa][0:64, 0:T], R=[pst[pa]], W=[tQTs[h % 2]])
        tt("dve", R1[:, 0:T], psb[pa][64:96, 0:T], COSS[:, 0:T], ALU.mult, R=[pst[pa], tSMP], W=[tR12])
        tt("dve", R2[:, 0:T], psb[pb2][64:96, 0:T], SINS[:, 0:T], ALU.mult, R=[pst[pb2], tSMP], W=[tR12])
        tt("dve", QT[64:96, c0:c0 + T], R1[:, 0:T], R2[:, 0:T], ALU.add, R=[tR12], W=[tQTs[h % 2]])

    def sample_tile(h, nxt):
        units = []
        posts = {}
        subs = {}
        for g in range(NKG + 1):
            if g < NKG:
                units.append((g * 512, 128, g * 4, 0, False, g))
                subs[len(units) - 1] = 4
            else:
                units.append((PAST, T, PAST // 128, 0, False, NKG))
            if nxt is not None:
                posts[len(units) - 1] = [lambda g=g: ktask_s(nxt, g), lambda g=g: vtask_s(nxt, g)]
        if nxt is not None:
            posts.setdefault(0, []).insert(0, lambda: qtask_s(nxt))
        return dict(qc0=(h % 2) * 64, nq=T, units=units, posts=posts, sub=subs, qtok=tQTs[h % 2],
                    gdst=(lambda vr, h=h: GS[vr, h // 2, 0:T]), gtok=tGS)

    rope_rows(kgroups_s)
    for g in range(NKG + 1):
        ktask_s(0, g)
        vtask_s(0, g)
    qtask_s(0)
    for h in range(NH):
        attend_prompt(h, [sample_tile(h, h + 1 if h + 1 < NH else None)], None)
    flush_pend(1 << 60)
    P.barrier()

    if stop == "C":
        return finish()
    AD = Alloc(DBASE)
    W1Z = AD.bf16(8 * 1024).rearrange("p (k n) -> p k n", k=8)
    WO1 = AD.bf16(8 * 1024).rearrange("p (k n) -> p k n", k=8)
    tWD = Tok()
    XDp = AD.p
    STD = [arena[:, XDp + i * 4096:XDp + (i + 1) * 4096] for i in range(2)]
    tSTD = [Tok(), Tok()]
    sSTD = [dsem("d_std0"), dsem("d_std1")]
    XD = [AD.f32(4096).rearrange("p (k n) -> p k n", k=8) for _ in range(2)]
    tXD = [Tok(), Tok()]
    sXD = [dsem("d_xd0"), dsem("d_xd1")]
    XND = [AD.bf16(4096).rearrange("p (k n) -> p k n", k=8) for _ in range(2)]
    tXND = [Tok(), Tok()]
    sXND = [dsem("d_xnd0"), dsem("d_xnd1")]
    ZD = [AD.bf16(512) for _ in range(2)]
    tZD = [Tok(), Tok()]
    SQD = AD.bf16(4096).rearrange("p (k n) -> p k n", k=8)
    tSQD = Tok()
    RSD = AD.f32(512)
    tRSD = Tok()

    for q in range(4):
        i = q % 2
        dma("sp", STD[i][:, 0:2048].rearrange("p (c n) -> p c n", c=2),
            w_in1[q * 256:(q + 1) * 256, 416:1440].rearrange("(c p) n -> p c n", p=128), R=[], W=[tSTD[i]], sem=sSTD[i])
        for cc in range(2):
            c = q * 2 + cc
            ts("dve", W1Z[:, c, :], STD[i][:, cc * 1024:(cc + 1) * 1024], g1[:, c:c + 1], None, ALU.mult, None,
               R=[tSTD[i], tVEC], W=[tWD])
    for q in range(2):
        i = q % 2
        dma("sp", STD[i].rearrange("p (c n) -> p c n", c=4), w_out1[q * 512:(q + 1) * 512, :].rearrange("(c p) n -> p c n", p=128),
            R=[], W=[tSTD[i]], sem=sSTD[i])
        for cc in range(4):
            (acp if cc % 2 else (lambda o, a, R, W: cp("dve", o, a, R, W)))(WO1[:, q * 4 + cc, :], STD[i][:, cc * 1024:(cc + 1) * 1024], R=[tSTD[i]], W=[tWD])

    P.barrier()
    dtiles = []
    for i in range(NO):
        dtiles.append(dict(n=512, x1n=x1ns[:, 2 * i * 512:(2 * i + 1) * 512], x1=x1s[:, i * 512:(i + 1) * 512],
                           g=(lambda c, i=i: G[:, c, i * 512:(i + 1) * 512]), gtok=tG[i], y=yT[:, i * 512:(i + 1) * 512]))
    dtiles.append(dict(n=T, x1n=x1ns[:, S:S + T], x1=x1s[:, SO:SO + T], g=(lambda c: GS[:, c, 0:T]), gtok=tGS, y=ysT[:, :]))
    zc = [0]

    def d_load_xnd(di):
        dt_ = dtiles[di]
        n = dt_["n"]
        b = di % 2
        dma("sp", XND[b][:, :, 0:n], dt_["x1n"].rearrange("(c p) n -> p c n", p=128), R=[tX1NS], W=[tXND[b]], sem=sXND[b])

    def d_load_xd(di):
        dt_ = dtiles[di]
        n = dt_["n"]
        b = di % 2
        dma("sp", XD[b][:, :, 0:n], dt_["x1"].rearrange("(c p) n -> p c n", p=128), R=[tX1S], W=[tXD[b]], sem=sXD[b])

    def d_load(di):
        d_load_xnd(di)
        d_load_xd(di)

    def d_zproj(di):
        dt_ = dtiles[di]
        n = dt_["n"]
        b = di % 2
        for c in range(8):
            pb = ps_next()
            for k in range(8):
                mm(psb[pb][:, 0:n], W1Z[:, k, c * 128:(c + 1) * 128], XND[b][:, k, 0:n], k == 0, k == 7, R=[tWD, tXND[b]], W=[pst[pb]])
            zb = zc[0] % 2
            zc[0] += 1
            act(ZD[zb][:, 0:n], psb[pb][:, 0:n], AF.Silu, R=[pst[pb]], W=[tZD[zb]])
            tt("dve", dt_["g"](c), dt_["g"](c), ZD[zb][:, 0:n], ALU.mult, R=[tZD[zb]], W=[dt_["gtok"]])

    def d_outproj(di):
        dt_ = dtiles[di]
        n = dt_["n"]
        b = di % 2
        for dch in range(8):
            pb = ps_next()
            for c in range(8):
                mm(psb[pb][:, 0:n], WO1[:, c, dch * 128:(dch + 1) * 128], dt_["g"](c), c == 0, c == 7, R=[tWD, dt_["gtok"]], W=[pst[pb]])
            tt("dve", XD[b][:, dch, 0:n], psb[pb][:, 0:n], XD[b][:, dch, 0:n], ALU.add, R=[pst[pb]], W=[tXD[b]])
        act(SQD[:, :, 0:n], XD[b][:, :, 0:n], AF.Square, R=[tXD[b]], W=[tSQD])

    def d_epilogue(di):
        dt_ = dtiles[di]
        n = dt_["n"]
        b = di % 2
        pss = sumsq(lambda c: SQD[:, c, 0:n], 8, n, tSQD)
        rstd_from_ps(pss, n, 1.0 / D, RSD, tRSD)
        for dch in range(8):
            stt("dve", XD[b][:, dch, 0:n], XD[b][:, dch, 0:n], gf[:, dch:dch + 1], RSD[:, 0:n], ALU.mult, ALU.mult,
                R=[tRSD, tVEC], W=[tXD[b]])
        dma("sp", dt_["y"].rearrange("(c p) n -> p c n", p=128), XD[b][:, :, 0:n], R=[tXD[b]], W=[tOUT], sem=sXD[b])

    d_load(0)
    if len(dtiles) > 1:
        d_load(1)
    d_zproj(0)
    for di in range(len(dtiles)):
        d_outproj(di)
        if di + 1 < len(dtiles):
            d_zproj(di + 1)
        if di + 2 < len(dtiles):
            d_load_xnd(di + 2)
        d_epilogue(di)
        if di + 2 < len(dtiles):
            d_load_xd(di + 2)

    return finish()


def _pc(v, k):
    return np.ascontiguousarray(v.reshape(k, 128).T)


def _pck(a):
    return np.ascontiguousarray(a.reshape(8, 128, 2).transpose(1, 0, 2).reshape(128, 16))


def _unpck(a):
    return np.ascontiguousarray(np.asarray(a).reshape(128, 8, 2).transpose(1, 0, 2).reshape(1024, 2).T)


def make_in_maps(inputs, S, PAST, T, n_cores):
    f32 = np.float32
    xp = np.asarray(inputs["x_prompt"], f32)
    xs = np.asarray(inputs["x_sample"], f32)
    sc = np.asarray(inputs["state_conv"], f32)
    cckv = np.asarray(inputs["cache_ckv"], f32)
    ckr = np.asarray(inputs["cache_krope"], f32)
    NS = S // 512
    inv = (1.0 / (np.float32(10000.0) ** (np.arange(0, 32, 2, dtype=f32) / np.float32(32)))).astype(f32)
    sel = np.zeros((33, 97), f32)
    for r in range(33):
        sel[r, 64 + r] = 1.0
    common = dict(
        sel=sel,
        w_in0=np.ascontiguousarray(inputs["conv_w_in"][0], f32),
        w_out0=np.ascontiguousarray(inputs["conv_w_out"][0], f32),
        w_in1=np.ascontiguousarray(inputs["mla_w_in"][0], f32),
        w_qb=np.ascontiguousarray(inputs["mla_w_qb"][0], f32),
        w_kvb=np.ascontiguousarray(inputs["mla_w_kvb"][0], f32),
        w_out1=np.ascontiguousarray(inputs["mla_w_out"][0], f32),
    )
    ng = np.asarray(inputs["norm_g"], f32)
    vec_base = np.zeros((128, 64), f32)
    vec_base[:, 0:8] = _pc(ng[0], 8)
    vec_base[:, 8:16] = _pc(ng[1], 8)
    vec_base[:, 16:24] = _pc(np.asarray(inputs["final_norm_g"], f32), 8)
    cwm = np.asarray(inputs["conv_w"], f32)[0]
    for k in range(3):
        vec_base[:, 24 + k:48:3] = _pc(cwm[k], 8)
    vec_base[:, 48:50] = _pc(np.asarray(inputs["mla_q_norm_g"], f32)[0], 2)
    vec_base[:, 50] = np.asarray(inputs["mla_kv_norm_g"], f32)[0]
    for p in range(96):
        vec_base[p, 51] = inv[(p % 32) % 16]
        vec_base[p, 53] = -1.0 if (p % 32) < 16 else 1.0
    maps = []
    for c in range(n_cores):
        b, par = c // 2, c % 2
        xb = xp[b]
        if par:
            xb = np.roll(xb, -512, axis=0)
            pos = np.roll(np.arange(S), -512)
            xh = xp[b, 510:512]
        else:
            pos = np.arange(S)
            xh = np.zeros((2, D), f32)
        pos = np.concatenate([pos, PAST + np.arange(T)]).astype(f32)
        kb = np.zeros((1, S), f32)
        if par == 0:
            kb[0, (NS - 1) * 512:] = NEG
        vec = vec_base.copy()
        vec[:, 52] = 1.0 if par == 0 else 0.0
        m = dict(common)
        m.update(
            xT=np.ascontiguousarray(xb.T), xhT=_pck(xh.T), xsT=np.ascontiguousarray(xs[c].T),
            scT=_pck(sc[0, c].T), cckvT=np.ascontiguousarray(cckv[0, c].T),
            ckrT=np.ascontiguousarray(ckr[0, c].T), posr=np.ascontiguousarray(np.broadcast_to(pos[None, :], (96, S + T))),
            kbias=kb, vecs=vec,
        )
        maps.append(m)
    return maps


def assemble(results, S, T, n_cores):
    f32 = np.float32
    NB = n_cores // 2
    NS = S // 512
    y_p = np.zeros((NB, S, D), f32)
    y_s = np.zeros((n_cores, T, D), f32)
    conv_p = np.zeros((1, NB, 2, D), f32)
    ckv_p = np.zeros((1, NB, S, 128), f32)
    kr_p = np.zeros((1, NB, S, 32), f32)
    conv_s = np.zeros((1, n_cores, 2, D), f32)
    ckv_s = np.zeros((1, n_cores, T, 128), f32)
    kr_s = np.zeros((1, n_cores, T, 32), f32)
    for c in range(n_cores):
        r = results[c]
        b, par = c // 2, c % 2
        yt = np.asarray(r["yT"])
        for i in range(NS // 2):
            g = 2 * i + par
            y_p[b, g * 512:(g + 1) * 512, :] = yt[:, i * 512:(i + 1) * 512].T
        y_s[c] = np.asarray(r["ysT"]).T
        conv_s[0, c] = _unpck(r["convs"])
        ckv_s[0, c] = np.asarray(r["ckvsT"]).T
        kr_s[0, c] = np.asarray(r["krsT"]).T
        if par == 0:
            conv_p[0, b] = _unpck(r["convp"])
            ckv_p[0, b] = np.asarray(r["ckvpT"]).T
            kr_p[0, b] = np.asarray(r["krpT"]).T
    return (y_p, y_s, conv_p, ckv_p, kr_p, conv_s, ckv_s, kr_s)


def kernel(**inputs):
    S, PAST, T, n = 8192, 4096, 64, 8
    nc = build(S, PAST, T)
    maps = make_in_maps(inputs, S, PAST, T, n)
    res = run_bass_kernel_spmd(nc, maps, core_ids=list(range(n)))
    return assemble(res.results, S, T, n)
```

```python
import math
import numpy as np
import concourse.bass as bass
import concourse.mybir as mybir
from concourse.bass_utils import run_bass_kernel_spmd

F32 = mybir.dt.float32
BF16 = mybir.dt.bfloat16
AF = mybir.ActivationFunctionType
ALU = mybir.AluOpType

D = 1024
NH = 16
EPS = 1e-6
SM_SCALE = 96 ** -0.5
MAGIC = 12582912.0
TWO_PI = 2.0 * math.pi
NEG = -30000.0


class Tok:
    __slots__ = ("w", "r")

    def __init__(self):
        self.w = []
        self.r = {}


class DSem:
    def __init__(self, h):
        self.h = h
        self.count = 0


class Prog:
    ENG = ("sp", "act", "dve", "pool", "pe")
    FENCE = ("act", "dve", "pool")

    def __init__(self):
        self.q = {e: [] for e in self.ENG}
        self.bar = {e: [] for e in self.ENG}

    def op(self, eng, fn, R=(), W=(), dsem=None):
        deps = list(self.bar[eng])
        self.bar[eng] = []
        for t in R:
            deps += t.w
        for t in W:
            deps += t.w
            deps += list(t.r.values())
        idx = len(self.q[eng])
        if dsem is not None:
            dsem.count += 16
            ref = ("dma", dsem, dsem.count)
        else:
            ref = ("op", eng, idx)
        self.q[eng].append(dict(fn=fn, deps=deps, dsem=dsem, sig=False, waits=[]))
        key = ("dma", id(dsem)) if dsem is not None else eng
        for t in R:
            t.r[key] = ref
        for t in W:
            t.w = [ref]
            t.r = {}
        return ref

    def barrier(self, extra_refs=()):
        refs = list(extra_refs)
        for e in self.ENG:
            if e == "sp":
                continue
            for i in range(len(self.q[e]) - 1, -1, -1):
                if self.q[e][i]["dsem"] is None and self.q[e][i]["fn"] is not None:
                    refs.append(("op", e, i))
                    break
        for e in self.ENG:
            self.bar[e] = list(refs)

    def finalize(self):
        for e in self.ENG:
            waited = {}
            for rec in self.q[e]:
                need = {}
                for ref in rec["deps"]:
                    if ref[0] == "op":
                        if ref[1] == e and e not in self.FENCE:
                            continue
                        k = ("op", ref[1])
                        v = ref[2]
                    else:
                        k = ("dma", ref[1])
                        v = ref[2]
                    if need.get(k, -1) < v:
                        need[k] = v
                for k, v in need.items():
                    if waited.get(k, -1) >= v:
                        continue
                    waited[k] = v
                    rec["waits"].append((k, v))
                    if k[0] == "op":
                        self.q[k[1]][v]["sig"] = True
        self.sigc = {}
        for e in self.ENG:
            c = 0
            m = {}
            for i, rec in enumerate(self.q[e]):
                if rec["sig"]:
                    c += 1
                    m[i] = c
            self.sigc[e] = m

    def check_deadlock(self):
        ptr = {e: 0 for e in self.ENG}
        semv = {e: 0 for e in self.ENG}
        dmav = {}
        progress = True
        while progress:
            progress = False
            for e in self.ENG:
                while ptr[e] < len(self.q[e]):
                    rec = self.q[e][ptr[e]]
                    ok = True
                    for k, v in rec["waits"]:
                        if k[0] == "op":
                            if semv[k[1]] < self.sigc[k[1]][v]:
                                ok = False
                        else:
                            if dmav.get(id(k[1]), 0) < v:
                                ok = False
                    if not ok:
                        break
                    if rec["dsem"] is not None:
                        dmav[id(rec["dsem"])] = dmav.get(id(rec["dsem"]), 0) + 16
                    elif rec["sig"] and rec["fn"] is not None:
                        semv[e] += 1
                    ptr[e] += 1
                    progress = True
        stuck = {e: (ptr[e], len(self.q[e])) for e in self.ENG if ptr[e] < len(self.q[e])}
        if stuck:
            msg = []
            for e, (p, n) in stuck.items():
                rec = self.q[e][p]
                msg.append((e, p, n, [(k[0], (k[1] if k[0] == "op" else "dma"), v) for k, v in rec["waits"]]))
            raise RuntimeError("deadlock in recorded program: %r" % (msg,))

    def replay(self, e, engine, sems):
        for rec in self.q[e]:
            for k, v in rec["waits"]:
                if k[0] == "op":
                    engine.wait_ge(sems[k[1]], self.sigc[k[1]][v])
                else:
                    engine.wait_ge(k[1].h, v)
            if rec["fn"] is None:
                continue
            ins = rec["fn"](engine)
            if rec["dsem"] is not None:
                ins.then_inc(rec["dsem"].h, 16)
            elif rec["sig"]:
                ins.then_inc(sems[e], 1)


def build(S=8192, PAST=4096, T=64, stop=None, WIDE=0, BCAST_DMA=True):
    NS = S // 512
    NO = NS // 2
    SO = S // 2
    nc = bass.Bass("TRN2", target_bir_lowering=False)

    def din(name, shape, dt=F32):
        return nc.dram_tensor(name, shape, dt, kind="ExternalInput").ap()

    def dout(name, shape):
        return nc.dram_tensor(name, shape, F32, kind="ExternalOutput").ap()

    xT = din("xT", [D, S])
    xhT = din("xhT", [128, 16])
    xsT = din("xsT", [D, T])
    scT = din("scT", [128, 16])
    cckvT = din("cckvT", [128, PAST])
    ckrT = din("ckrT", [32, PAST])
    posr = din("posr", [96, S + T])
    kbias = din("kbias", [1, S])
    vecs = din("vecs", [128, 64])
    sel_d = din("sel", [33, 97])
    w_in0 = din("w_in0", [D, 4096])
    w_out0 = din("w_out0", [D, D])
    w_in1 = din("w_in1", [D, 1440])
    w_qb = din("w_qb", [256, 1536])
    w_kvb = din("w_kvb", [128, 2048])
    w_out1 = din("w_out1", [D, D])

    yT = dout("yT", [D, SO])
    ysT = dout("ysT", [D, T])
    convp = dout("convp", [128, 16])
    ckvpT = dout("ckvpT", [128, S])
    krpT = dout("krpT", [32, S])
    convs = dout("convs", [128, 16])
    ckvsT = dout("ckvsT", [128, T])
    krsT = dout("krsT", [32, T])

    x1s = nc.dram_tensor("x1s", [D, SO + T], F32, kind="Internal").ap()
    x1ns = nc.dram_tensor("x1ns", [D, S + T], BF16, kind="Internal").ap()
    rrs = nc.dram_tensor("rrs", [2, 512], F32, kind="Internal").ap()

    P = Prog()
    tOUT = Tok()
    import contextlib
    es = contextlib.ExitStack()
    NW = 51200
    arena = es.enter_context(nc.sbuf_tensor("arena", [128, NW], F32))
    psb = [es.enter_context(nc.psum_tensor(f"ps{i}", [128, 512], F32)) for i in range(8)]
    pst = [Tok() for _ in range(8)]

    class Alloc:
        def __init__(self, base):
            self.p = base

        def f32(self, n, parts=(0, 128)):
            a = arena[parts[0]:parts[1], self.p:self.p + n]
            self.p += n
            assert self.p <= NW, self.p
            return a

        def bf16(self, n, parts=(0, 128)):
            w = (n + 1) // 2
            a = arena[parts[0]:parts[1], self.p:self.p + w].bitcast(BF16)
            self.p += w
            assert self.p <= NW, self.p
            return a[:, 0:n]

    def dsem(name):
        return DSem(es.enter_context(nc.semaphore(name)))

    AP_ = Alloc(0)
    VEC = AP_.f32(64)
    tVEC = Tok()
    g0, g1, gf = VEC[:, 0:8], VEC[:, 8:16], VEC[:, 16:24]
    cw = VEC[:, 24:48]
    gq = VEC[:, 48:50]
    gkv = VEC[:, 50:51]
    invf = VEC[0:96, 51:52]
    flagA = VEC[:, 52:53]
    sgn = VEC[0:96, 53:54]
    ONES = AP_.bf16(128)
    ONESF = AP_.f32(128)
    EPSB = AP_.f32(2)
    WK1 = AP_.bf16(16 * 97).rearrange("p (h c) -> p h c", h=16)
    WV = AP_.bf16(16 * 64).rearrange("p (h c) -> p h c", h=16)
    WQ = AP_.bf16(2 * 16 * 97).rearrange("p (k h c) -> p k h c", k=2, h=16)
    WQR = AP_.bf16(2 * 16 * 96).rearrange("p (k h c) -> p k h c", k=2, h=16)
    SEL = AP_.bf16(98, parts=(0, 33))[:, 0:97]
    W1R = AP_.bf16(8 * 32).rearrange("p (k c) -> p k c", k=8)
    tW = Tok()
    PBASE = AP_.p

    AR = Alloc(PBASE)
    GBASE = AR.p
    G = AR.bf16(8 * SO).rearrange("p (k n) -> p k n", k=8)
    GS = AR.bf16(8 * T).rearrange("p (k n) -> p k n", k=8)
    DBASE = AR.p
    CKVp = AR.p
    CKV = AR.bf16(S)
    KRp = AR.p
    KR = arena[0:33, KRp:KRp + S // 2].bitcast(BF16)
    COS = arena[64:96, KRp:KRp + SO // 2].bitcast(BF16)
    SIN = arena[64:96, KRp + SO // 2:KRp + S // 2].bitcast(BF16)
    AR.p += S // 2
    QN = AR.bf16(2 * SO).rearrange("p (k n) -> p k n", k=2)
    QNS = AR.bf16(2 * T).rearrange("p (k n) -> p k n", k=2)
    smp = AR.p
    COSS = arena[64:96, smp:smp + T // 2].bitcast(BF16)
    SINS = arena[64:96, smp + T // 2:smp + T].bitcast(BF16)
    KRN = arena[0:32, smp:smp + T // 2].bitcast(BF16)
    AR.p += T
    CKVN = AR.bf16(T)
    RBASE = AR.p
    tCKV = [Tok() for _ in range(max(NS, PAST // 512 + 1))]
    tKR = [Tok() for _ in range(max(NS, PAST // 512 + 1))]
    tQN = [Tok() for _ in range(NO)]
    tG = [Tok() for _ in range(NO)]
    tCS = [Tok() for _ in range(NO)]
    tSMP = Tok()
    tGS = Tok()

    sems = {}
    for e in ("act", "dve", "pool", "pe"):
        sems[e] = es.enter_context(nc.semaphore("s_" + e))

    def mm(out, lhsT, rhs, start, stop, R, W):
        return P.op("pe", lambda e: e.matmul(out, lhsT, rhs, start=start, stop=stop), R=R, W=W)

    psrr = [0]

    def ps_next():
        i = psrr[0] % 8
        psrr[0] += 1
        return i

    outrefs = []

    def dma(eng, out, in_, R, W, sem):
        isout = tOUT in W
        W = [t for t in W if t is not tOUT]
        ref = P.op(eng, lambda e: e.dma_start(out=out, in_=in_), R=R, W=W, dsem=sem)
        if isout:
            outrefs.append(ref)
        return ref

    def tt(eng, out, a, b, op, R, W):
        return P.op(eng, lambda e: e.tensor_tensor(out, a, b, op), R=R, W=W)

    def ts(eng, out, a, s1, s2, op0, op1, R, W):
        if s2 is None:
            return P.op(eng, lambda e: e.tensor_scalar(out, a, s1, None, op0), R=R, W=W)
        return P.op(eng, lambda e: e.tensor_scalar(out, a, s1, s2, op0, op1), R=R, W=W)

    def stt(eng, out, a, s, b, op0, op1, R, W):
        return P.op(eng, lambda e: e.scalar_tensor_tensor(out, a, s, b, op0, op1), R=R, W=W)

    def cp(eng, out, a, R, W):
        return P.op(eng, lambda e: e.tensor_copy(out, a), R=R, W=W)

    def act(out, a, func, R, W, scale=None):
        if scale is None:
            return P.op("act", lambda e: e.activation(out=out, in_=a, func=func), R=R, W=W)
        return P.op("act", lambda e: e.activation(out=out, in_=a, func=func, scale=scale), R=R, W=W)

    def acp(out, a, R, W):
        return P.op("act", lambda e: e.copy(out, a), R=R, W=W)

    def mset(eng, out, v, R, W):
        return P.op(eng, lambda e: e.memset(out, v), R=R, W=W)

    def rstd_from_ps(ps_i, n, inv_cnt, RSout, tRS):
        P.op("act", lambda e: e.activation(out=RSout[:, 0:n], in_=psb[ps_i][:, 0:n], func=AF.Ln, scale=inv_cnt, bias=EPSB[:, 0:1]),
             R=[pst[ps_i], tW], W=[tRS])
        P.op("act", lambda e: e.activation(out=RSout[:, 0:n], in_=RSout[:, 0:n], func=AF.Exp, scale=-0.5), R=[], W=[tRS])

    def sumsq(src3, nchunks, n, tsrc, parts=128):
        b = ps_next()
        for c in range(nchunks):
            mm(psb[b][:, 0:n], ONES[0:parts, :], src3(c), c == 0, c == nchunks - 1, R=[tsrc, tW], W=[pst[b]])
        return b

    def finish():
        P.bar["sp"] = P.bar["sp"] + outrefs
        P.op("sp", None)
        P.bar["pool"] = P.bar["pool"] + outrefs
        P.op("pool", None)
        P.finalize()
        P.check_deadlock()

        with nc.Block() as block:
            @block.sync
            def _(eng):
                P.replay("sp", eng, sems)

            @block.scalar
            def _(eng):
                P.replay("act", eng, sems)

            @block.vector
            def _(eng):
                P.replay("dve", eng, sems)

            @block.gpsimd
            def _(eng):
                P.replay("pool", eng, sems)

            @block.tensor
            def _(eng):
                P.replay("pe", eng, sems)
        es.close()
        return nc

    s_vec = dsem("d_vec")
    dma("sp", VEC, vecs[:, :], R=[], W=[tVEC], sem=s_vec)
    mset("dve", ONES, 1.0, R=[], W=[tW])
    mset("dve", ONESF, 1.0, R=[], W=[tW])
    mset("dve", EPSB, EPS, R=[], W=[tW])

    A1 = Alloc(PBASE)
    W0 = A1.bf16(8 * 4096).rearrange("p (k n) -> p k n", k=8)
    WO0 = A1.bf16(8 * 1024).rearrange("p (k n) -> p k n", k=8)
    tW0 = Tok()
    XBp = A1.p
    XB = [A1.f32(4096).rearrange("p (k n) -> p k n", k=8) for _ in range(2)]
    tXB = [Tok(), Tok()]
    sXB = [dsem("d_xb0"), dsem("d_xb1")]
    ST = [arena[:, XBp + i * 4096:XBp + (i + 1) * 4096] for i in range(2)]
    tST = [Tok(), Tok()]
    sST = [dsem("d_st0"), dsem("d_st1")]
    XSQ = A1.bf16(4096).rearrange("p (k n) -> p k n", k=8)
    tXSQ = Tok()
    XN = [A1.bf16(4096).rearrange("p (k n) -> p k n", k=8) for _ in range(2)]
    tXN = [Tok(), Tok()]
    sXN = [dsem("d_xn0"), dsem("d_xn1")]
    RS = [A1.f32(512) for _ in range(2)]
    tRS = [Tok(), Tok()]
    ZS = [A1.f32(512) for _ in range(2)]
    CS = [A1.f32(512) for _ in range(2)]
    GB = [A1.f32(512) for _ in range(2)]
    CV = [A1.f32(512) for _ in range(2)]
    U = [A1.f32(514) for _ in range(2)]
    tCH = [Tok(), Tok()]
    tZS = [Tok(), Tok()]
    tCSb = [Tok(), Tok()]
    tGB = [Tok(), Tok()]
    tCV = [Tok(), Tok()]
    tU = [Tok(), Tok()]
    HALO = A1.f32(16).rearrange("p (k n) -> p k n", k=8)
    tHALO = Tok()
    sHALO = dsem("d_halo")
    GA = [A1.bf16(4096).rearrange("p (k n) -> p k n", k=8) for _ in range(2)]
    tGA = [Tok(), Tok()]
    A1END = A1.p

    stc = [0]

    def stage_load(src_ap, view):
        i = stc[0] % 2
        stc[0] += 1
        dma("sp", view(ST[i]), src_ap, R=[], W=[tST[i]], sem=sST[i])
        return i

    engs2 = ["dve", "pool"]
    for c in range(8):
        i = stage_load(w_in0[c * 128:(c + 1) * 128, :], lambda a: a)
        for hlf in range(2):
            sl = slice(hlf * 2048, (hlf + 1) * 2048)
            ts("dve", W0[:, c, sl], ST[i][:, sl], g0[:, c:c + 1], None, ALU.mult, None, R=[tST[i], tVEC], W=[tW0])
    for q in range(2):
        i = stage_load(w_out0[q * 512:(q + 1) * 512, :].rearrange("(c p) n -> p c n", p=128),
                       lambda a: a.rearrange("p (c n) -> p c n", c=4))
        for cc in range(4):
            (acp if cc % 2 else (lambda o, a, R, W: cp("dve", o, a, R, W)))(WO0[:, q * 4 + cc, :], ST[i][:, cc * 1024:(cc + 1) * 1024], R=[tST[i]], W=[tW0])
    i = stage_load(w_in1[:, 384:416].rearrange("(c p) n -> p c n", p=128), lambda a: a[:, 0:256].rearrange("p (c n) -> p c n", c=8))
    for c in range(8):
        ts("dve", W1R[:, c, 0:16], ST[i][:, c * 32 + 16:c * 32 + 32], g1[:, c:c + 1], None, ALU.mult, None, R=[tST[i], tVEC], W=[tW])
        ts("dve", W1R[:, c, 16:32], ST[i][:, c * 32:c * 32 + 16], g1[:, c:c + 1], None, ALU.mult, None, R=[tST[i], tVEC], W=[tW])
    mset("pool", WQ.rearrange("p k h c -> p (k h c)"), 0.0, R=[], W=[tW])
    mset("pool", WQR.rearrange("p k h c -> p (k h c)"), 0.0, R=[], W=[tW])
    i = stage_load(w_qb.rearrange("(c p) n -> p c n", p=128), lambda a: a[:, 0:3072].rearrange("p (c n) -> p c n", c=2))
    for c in range(2):
        src = ST[i][:, c * 1536:(c + 1) * 1536].rearrange("p (h d) -> p h d", h=16)
        cp("dve", WQ[:, c, :, 0:96], src, R=[tST[i]], W=[tW])
        cp("pool", WQR[:, c, :, 64:80], src[:, :, 80:96], R=[tST[i]], W=[tW])
        cp("pool", WQR[:, c, :, 80:96], src[:, :, 64:80], R=[tST[i]], W=[tW])
    mset("pool", WK1.rearrange("p h c -> p (h c)"), 0.0, R=[], W=[tW])
    i = stage_load(w_kvb[:, :], lambda a: a[:, 0:2048])
    src = ST[i][:, 0:2048].rearrange("p (h d) -> p h d", h=16)
    cp("dve", WK1[:, :, 0:64], src[:, :, 0:64], R=[tST[i]], W=[tW])
    cp("pool", WV[:, :, :], src[:, :, 64:128], R=[tST[i]], W=[tW])
    i = stage_load(sel_d[:, :], lambda a: a[0:33, 0:97])
    cp("dve", SEL, ST[i][0:33, 0:97], R=[tST[i]], W=[tW])
    P.barrier()

    if stop == "W":
        return finish()
    def s1_load(si, src, n):
        b = si % 2
        srcv = src.rearrange("p (c k) -> p c k", k=2) if src is xhT else src.rearrange("(c p) n -> p c n", p=128)
        dma("sp", XB[b][:, :, 0:n], srcv, R=[], W=[tXB[b]], sem=sXB[b])

    def s1_square(si, n):
        b = si % 2
        m = max(n, WIDE)
        act(XSQ[:, :, 0:m], XB[b][:, :, 0:m], AF.Square, R=[tXB[b]], W=[tXSQ])

    def s1_stats(si, n):
        b = si % 2
        m = max(n, WIDE)
        pb = sumsq(lambda c: XSQ[:, c, 0:m], 8, m, tXSQ)
        rstd_from_ps(pb, m, 1.0 / D, RS[b], tRS[b])

    def s1_norm(si, n, eng="dve"):
        b = si % 2
        m = max(n, WIDE)
        tt(eng, XN[b][:, :, 0:m], XB[b][:, :, 0:m], RS[b][:, 0:m].unsqueeze(1).broadcast_to([128, 8, m]), ALU.mult,
           R=[tXB[b], tRS[b]], W=[tXN[b]])

    def a1_stage1(si, src, n):
        s1_load(si, src, n)
        s1_square(si, n)
        s1_stats(si, n)
        s1_norm(si, n, "dve")

    chc = [0]

    def a1_chunk(si, j, n, full=True):
        b = si % 2
        m = max(n, WIDE)
        cb = chc[0] % 2
        chc[0] += 1
        banks = {}
        roles = (("c", 1024), ("xv", 2048), ("b", 0), ("z", 3072)) if full else (("c", 1024), ("xv", 2048))
        for role, cb0 in roles:
            pb = ps_next()
            banks[role] = pb
            col = cb0 + j * 128
            for k in range(8):
                mm(psb[pb][:, 0:m], W0[:, k, col:col + 128], XN[b][:, k, 0:m], k == 0, k == 7, R=[tW0, tXN[b]], W=[pst[pb]])
        P.op("act", lambda e, o=CS[cb][:, 0:m], a=psb[banks["c"]][:, 0:m]: e.copy(o, a), R=[pst[banks["c"]]], W=[tCSb[cb]])
        cp("dve", U[cb][:, 0:2], HALO[:, j, :], R=[tHALO], W=[tU[cb]])
        tt("dve", U[cb][:, 2:2 + m], psb[banks["xv"]][:, 0:m], CS[cb][:, 0:m], ALU.mult, R=[pst[banks["xv"]], tCSb[cb]], W=[tU[cb]])
        cp("dve", HALO[:, j, :], U[cb][:, n:n + 2], R=[tU[cb]], W=[tHALO])
        if not full:
            return
        act(ZS[cb][:, 0:m], psb[banks["z"]][:, 0:m], AF.Silu, R=[pst[banks["z"]]], W=[tZS[cb]])
        tt("dve", GB[cb][:, 0:m], psb[banks["b"]][:, 0:m], ZS[cb][:, 0:m], ALU.mult, R=[pst[banks["b"]], tZS[cb]], W=[tGB[cb]])
        ts("dve", CV[cb][:, 0:m], U[cb][:, 0:m], cw[:, 3 * j:3 * j + 1], None, ALU.mult, None, R=[tU[cb], tVEC], W=[tCV[cb]])
        stt("dve", CV[cb][:, 0:m], U[cb][:, 1:1 + m], cw[:, 3 * j + 1:3 * j + 2], CV[cb][:, 0:m], ALU.mult, ALU.add, R=[tU[cb]], W=[tCV[cb]])
        stt("dve", CV[cb][:, 0:m], U[cb][:, 2:2 + m], cw[:, 3 * j + 2:3 * j + 3], CV[cb][:, 0:m], ALU.mult, ALU.add, R=[tU[cb]], W=[tCV[cb]])
        tt("dve", GA[b][:, j, 0:m], GB[cb][:, 0:m], CV[cb][:, 0:m], ALU.mult, R=[tGB[cb], tCV[cb]], W=[tGA[b]])

    def s3a(si, n, x1_dst=None):
        b = si % 2
        m = max(n, WIDE)
        for dch in range(8):
            pb = ps_next()
            for k in range(8):
                mm(psb[pb][:, 0:m], WO0[:, k, dch * 128:(dch + 1) * 128], GA[b][:, k, 0:m], k == 0, k == 7, R=[tW0, tGA[b]], W=[pst[pb]])
            tt("dve", XB[b][:, dch, 0:m], psb[pb][:, 0:m], XB[b][:, dch, 0:m], ALU.add, R=[pst[pb]], W=[tXB[b]])
        act(XSQ[:, :, 0:m], XB[b][:, :, 0:m], AF.Square, R=[tXB[b]], W=[tXSQ])
        if x1_dst is not None:
            dma("sp", x1_dst.rearrange("(c p) n -> p c n", p=128), XB[b][:, :, 0:n], R=[tXB[b]], W=[tX1S], sem=sXB[b])

    def s3b(si, n, x1n_dst, x1_dst):
        b = si % 2
        m = max(n, WIDE)
        pb = sumsq(lambda c: XSQ[:, c, 0:m], 8, m, tXSQ)
        rstd_from_ps(pb, m, 1.0 / D, RS[b], tRS[b])
        tt("dve", XN[b][:, :, 0:m], XB[b][:, :, 0:m], RS[b][:, 0:m].unsqueeze(1).broadcast_to([128, 8, m]), ALU.mult,
           R=[tXB[b], tRS[b]], W=[tXN[b]])
        dma("sp", x1n_dst.rearrange("(c p) n -> p c n", p=128), XN[b][:, :, 0:n], R=[tXN[b]], W=[tX1NS], sem=sXN[b])

    tX1NS = Tok()
    tX1S = Tok()
    print("sbuf words: A1 end", A1END, "RBASE", RBASE, "remaining bytes", nc.sbuf_bytes_remaining)
    slots = []
    for s in range(NS):
        slots.append(dict(src=xT[:, s * 512:(s + 1) * 512], n=512, own=(s // 2 if s % 2 == 0 else None),
                          x1n=x1ns[:, s * 512:(s + 1) * 512],
                          x1=(x1s[:, (s // 2) * 512:(s // 2 + 1) * 512] if s % 2 == 0 else None)))
    slots.append(dict(src=xsT[:, :], n=T, own=NO, x1n=x1ns[:, S:S + T], x1=x1s[:, SO:SO + T]))

    mset("pool", HALO.rearrange("p k n -> p (k n)"), 0.0, R=[], W=[tHALO])
    a1_stage1(1, xhT, 2)
    for j in range(8):
        a1_chunk(1, j, 2, full=False)
    if stop == "A1a":
        return finish()
    a1_stage1(0, slots[0]["src"], slots[0]["n"])
    s1_load(1, slots[1]["src"], slots[1]["n"])
    L_ = len(slots)

    def pre_slot(si):
        if si == NS - 1:
            ts("dve", HALO.rearrange("p k n -> p (k n)"), HALO.rearrange("p k n -> p (k n)"), flagA, None, ALU.mult, None,
               R=[tVEC], W=[tHALO])
        if si == NS:
            dma("sp", convp.rearrange("p (c k) -> p c k", k=2), HALO, R=[tHALO], W=[tOUT], sem=sHALO)
            dma("sp", HALO, scT.rearrange("p (c k) -> p c k", k=2), R=[], W=[tHALO], sem=sHALO)

    pre_slot(0)
    a1_chunk(0, 0, slots[0]["n"])
    for si, sd in enumerate(slots):
        n = sd["n"]
        nx = slots[si + 1] if si + 1 < L_ else None
        for j in range(1, 8):
            a1_chunk(si, j, n)
            if j == 1 and si >= 1:
                pv = slots[si - 1]
                s3b(si - 1, pv["n"], pv["x1n"], pv["x1"])
                if nx is not None:
                    s1_load(si + 1, nx["src"], nx["n"])
            if nx is not None:
                if j == 3:
                    s1_square(si + 1, nx["n"])
                if j == 5:
                    s1_stats(si + 1, nx["n"])
                if j == 6:
                    s1_norm(si + 1, nx["n"])
        if nx is not None:
            pre_slot(si + 1)
            a1_chunk(si + 1, 0, nx["n"])
        s3a(si, n, sd["x1"])
        if si == L_ - 1:
            s3b(si, n, sd["x1n"], sd["x1"])
    dma("sp", convs.rearrange("p (c k) -> p c k", k=2), HALO, R=[tHALO], W=[tOUT], sem=sHALO)
    P.barrier([("dma", sXB[0], sXB[0].count), ("dma", sXB[1], sXB[1].count), ("dma", sXN[0], sXN[0].count),
               ("dma", sXN[1], sXN[1].count), ("dma", sHALO, sHALO.count)])

    if stop == "A1":
        return finish()
    A2 = Alloc(RBASE)
    A2g = Alloc(GBASE) if DBASE - GBASE >= 12700 else A2
    W1 = A2g.bf16(8 * 416).rearrange("p (k n) -> p k n", k=8)
    tW1 = Tok()
    ST2 = [A2g.f32(2880) for _ in range(2)]
    tST2 = [Tok(), Tok()]
    sST2 = [dsem("d_st2a"), dsem("d_st2b")]
    X1N = [A2g.bf16(4096).rearrange("p (k n) -> p k n", k=8) for _ in range(2)]
    tX1N = [Tok(), Tok()]
    sX1N = [dsem("d_x1n0"), dsem("d_x1n1")]
    POS = [A2g.f32(512, parts=(0, 96)) for _ in range(2)]
    assert A2g is A2 or A2g.p <= DBASE
    tPOS = [Tok(), Tok()]
    sPOS = [dsem("d_pos0"), dsem("d_pos1")]
    KBFb = [A2.f32(min(2048, S), parts=(32, 33)) for _ in range(2)]
    tKBFb = [Tok(), Tok()]
    sKBFb = [dsem("d_kbf0"), dsem("d_kbf1")]
    ANG = A2.f32(512, parts=(0, 96))
    T1 = A2.f32(512, parts=(0, 96))
    RSN = A2.f32(512, parts=(0, 96))
    RC = A2.f32(512, parts=(0, 96))
    SINTb = [A2.f32(512, parts=(0, 96)) for _ in range(2)]
    COSTb = [A2.f32(512, parts=(0, 96)) for _ in range(2)]
    tTR = Tok()
    tSCb = [Tok(), Tok()]
    CKVF = [A2.f32(512) for _ in range(2)]
    tCKVF = [Tok(), Tok()]
    sCKVF = [dsem("d_ckvf0"), dsem("d_ckvf1")]
    KRF = [A2.f32(512, parts=(0, 32)) for _ in range(2)]
    tKRF = [Tok(), Tok()]
    sKRF = [dsem("d_krf0"), dsem("d_krf1")]
    KTMP = A2.f32(512, parts=(0, 32))
    tKTMP = Tok()
    KVSQ = A2.bf16(512)
    tKVSQ = Tok()
    RSK = A2.f32(512)
    tRSK = Tok()
    QLF = A2.f32(1024).rearrange("p (k n) -> p k n", k=2)
    tQLF = Tok()
    QSQ = A2.bf16(1024).rearrange("p (k n) -> p k n", k=2)
    tQSQ = Tok()
    RSQ = A2.f32(512)
    tRSQ = Tok()

    for q in range(4):
        i = q % 2
        dma("sp", ST2[i].rearrange("p (c n) -> p c n", c=2), w_in1[q * 256:(q + 1) * 256, :].rearrange("(c p) n -> p c n", p=128),
            R=[], W=[tST2[i]], sem=sST2[i])
        for cc in range(2):
            c = q * 2 + cc
            ts("dve", W1[:, c, :], ST2[i][:, cc * 1440:cc * 1440 + 416], g1[:, c:c + 1], None, ALU.mult, None,
               R=[tST2[i], tVEC], W=[tW1])

    def a2_load(si, sd):
        n = sd["n"]
        b = si % 2
        is_s = si == NS
        pc = S if is_s else si * 512
        dma("sp", X1N[b][:, :, 0:n], sd["x1n"].rearrange("(c p) n -> p c n", p=128), R=[tX1NS], W=[tX1N[b]], sem=sX1N[b])
        dma("sp", POS[b][:, 0:n], posr[:, pc:pc + n], R=[], W=[tPOS[b]], sem=sPOS[b])

    def a2_trig(si, sd):
        n = sd["n"]
        m = max(n, WIDE)
        b = si % 2
        own = sd["own"]
        is_s = si == NS
        ts("dve", ANG[:, 0:m], POS[b][:, 0:m], invf, None, ALU.mult, None, R=[tPOS[b], tVEC], W=[tTR])
        ts("dve", T1[:, 0:m], ANG[:, 0:m], 1.0 / TWO_PI, MAGIC, ALU.mult, ALU.add, R=[], W=[tTR])
        ts("dve", T1[:, 0:m], T1[:, 0:m], MAGIC, -TWO_PI, ALU.subtract, ALU.mult, R=[], W=[tTR])
        tt("dve", RSN[:, 0:m], T1[:, 0:m], ANG[:, 0:m], ALU.add, R=[tSCb[b]], W=[tTR])
        ts("dve", RC[:, 0:m], RSN[:, 0:m], math.pi / 2, None, ALU.add, None, R=[], W=[tTR])
        ts("dve", T1[:, 0:m], RC[:, 0:m], math.pi, TWO_PI, ALU.is_gt, ALU.mult, R=[], W=[tTR])
        tt("dve", RC[:, 0:m], RC[:, 0:m], T1[:, 0:m], ALU.subtract, R=[], W=[tTR])
        for XX in (RSN, RC):
            ts("dve", XX[:, 0:m], XX[:, 0:m], -3.14159, None, ALU.max, None, R=[], W=[tTR])
            ts("dve", XX[:, 0:m], XX[:, 0:m], 3.14159, None, ALU.min, None, R=[], W=[tTR])
        ts("dve", RSN[:, 0:m], RSN[:, 0:m], sgn, None, ALU.mult, None, R=[tVEC], W=[tTR])
        act(SINTb[b][:, 0:m], RSN[:, 0:m], AF.Sin, R=[tTR], W=[tSCb[b]])
        act(COSTb[b][:, 0:m], RC[:, 0:m], AF.Sin, R=[tTR], W=[tSCb[b]])
        if own is not None:
            if is_s:
                acp(COSS[:, 0:n], COSTb[b][64:96, 0:n], R=[tSCb[b]], W=[tSMP])
                acp(SINS[:, 0:n], SINTb[b][64:96, 0:n], R=[tSCb[b]], W=[tSMP])
            else:
                acp(COS[:, own * 512:own * 512 + n], COSTb[b][64:96, 0:n], R=[tSCb[b]], W=[tCS[own]])
                acp(SIN[:, own * 512:own * 512 + n], SINTb[b][64:96, 0:n], R=[tSCb[b]], W=[tCS[own]])

    a2_deferred = []

    def a2_slot(si, sd):
        n = sd["n"]
        m = max(n, WIDE)
        b = si % 2
        own = sd["own"]
        is_s = si == NS
        pc = S if is_s else si * 512
        SINT, COST, tSC = SINTb[b], COSTb[b], tSCb[b]
        pkv = ps_next()
        for k in range(8):
            mm(psb[pkv][:, 0:m], W1[:, k, 256:384], X1N[b][:, k, 0:m], k == 0, k == 7, R=[tW1, tX1N[b]], W=[pst[pkv]])
        pkr = ps_next()
        for k in range(8):
            mm(psb[pkr][0:32, 0:m], W1[:, k, 384:416], X1N[b][:, k, 0:m], k == 0, k == 7, R=[tW1, tX1N[b]], W=[pst[pkr]])
        pkq = ps_next()
        for k in range(8):
            mm(psb[pkq][0:32, 0:m], W1R[:, k, :], X1N[b][:, k, 0:m], k == 0, k == 7, R=[tW, tX1N[b]], W=[pst[pkq]])
        cp("dve", CKVF[b][:, 0:m], psb[pkv][:, 0:m], R=[pst[pkv]], W=[tCKVF[b]])
        act(KVSQ[:, 0:m], psb[pkv][:, 0:m], AF.Square, R=[pst[pkv]], W=[tKVSQ])
        pss = sumsq(lambda c: KVSQ[:, 0:m], 1, m, tKVSQ)
        rstd_from_ps(pss, m, 1.0 / 128, RSK, tRSK)
        stt("dve", CKVF[b][:, 0:m], CKVF[b][:, 0:m], gkv, RSK[:, 0:m], ALU.mult, ALU.mult, R=[tRSK, tVEC], W=[tCKVF[b]])
        if is_s:
            acp(CKVN[:, 0:n], CKVF[b][:, 0:n], R=[tCKVF[b]], W=[tSMP])
            a2_deferred.append(lambda b=b, n=n: dma("sp", ckvsT[:, 0:n], CKVF[b][:, 0:n], R=[tCKVF[b]], W=[tOUT], sem=sCKVF[b]))
        else:
            acp(CKV[:, pc:pc + n], CKVF[b][:, 0:n], R=[tCKVF[b]], W=[tCKV[si]])
            a2_deferred.append(lambda b=b, n=n, pc=pc: dma("sp", ckvpT[:, pc:pc + n], CKVF[b][:, 0:n], R=[tCKVF[b]], W=[tOUT], sem=sCKVF[b]))
        tt("dve", KRF[b][:, 0:m], psb[pkr][0:32, 0:m], COST[0:32, 0:m], ALU.mult, R=[pst[pkr], tSC], W=[tKRF[b]])
        tt("dve", KTMP[:, 0:m], psb[pkq][0:32, 0:m], SINT[0:32, 0:m], ALU.mult, R=[pst[pkq], tSC], W=[tKTMP])
        tt("dve", KRF[b][:, 0:m], KRF[b][:, 0:m], KTMP[:, 0:m], ALU.add, R=[tKTMP], W=[tKRF[b]])
        if is_s:
            acp(KRN[:, 0:n], KRF[b][:, 0:n], R=[tKRF[b]], W=[tSMP])
            a2_deferred.append(lambda b=b, n=n: dma("sp", krsT[:, 0:n], KRF[b][:, 0:n], R=[tKRF[b]], W=[tOUT], sem=sKRF[b]))
        else:
            acp(KR[0:32, pc:pc + n], KRF[b][:, 0:n], R=[tKRF[b]], W=[tKR[si]])
            a2_deferred.append(lambda b=b, n=n, pc=pc: dma("sp", krpT[:, pc:pc + n], KRF[b][:, 0:n], R=[tKRF[b]], W=[tOUT], sem=sKRF[b]))
        if own is not None:
            pq = []
            for c in range(2):
                pb = ps_next()
                pq.append(pb)
                for k in range(8):
                    mm(psb[pb][:, 0:m], W1[:, k, c * 128:(c + 1) * 128], X1N[b][:, k, 0:m], k == 0, k == 7, R=[tW1, tX1N[b]], W=[pst[pb]])
                cp("dve", QLF[:, c, 0:m], psb[pb][:, 0:m], R=[pst[pb]], W=[tQLF])
            act(QSQ[:, :, 0:m], QLF[:, :, 0:m], AF.Square, R=[tQLF], W=[tQSQ])
            pss = sumsq(lambda c: QSQ[:, c, 0:m], 2, m, tQSQ)
            rstd_from_ps(pss, m, 1.0 / 256, RSQ, tRSQ)
            for c in range(2):
                if is_s:
                    dst, tk = QNS[:, c, 0:n], tSMP
                else:
                    dst, tk = QN[:, c, own * 512:own * 512 + n], tQN[own]
                stt("dve", dst, QLF[:, c, 0:n], gq[:, c:c + 1], RSQ[:, 0:n], ALU.mult, ALU.mult, R=[tQLF, tRSQ, tVEC], W=[tk])

    a2_load(0, slots[0])
    a2_trig(0, slots[0])
    KW = min(2048, S)
    for g in range(S // KW):
        kb_ = KBFb[g % 2]
        dma("sp", kb_[:, 0:KW], kbias[:, g * KW:(g + 1) * KW], R=[], W=[tKBFb[g % 2]], sem=sKBFb[g % 2])
        cp("dve", KR[32:33, g * KW:(g + 1) * KW], kb_[:, 0:KW], R=[tKBFb[g % 2]], W=[tKR[i_] for i_ in range(g * KW // 512, (g + 1) * KW // 512)])
    for si, sd in enumerate(slots):
        if si + 1 < len(slots):
            a2_load(si + 1, slots[si + 1])
        while a2_deferred:
            a2_deferred.pop(0)()
        a2_slot(si, sd)
        if si + 1 < len(slots):
            a2_trig(si + 1, slots[si + 1])
    while a2_deferred:
        a2_deferred.pop(0)()
    P.barrier([("dma", sCKVF[0], sCKVF[0].count), ("dma", sCKVF[1], sCKVF[1].count),
               ("dma", sKRF[0], sKRF[0].count), ("dma", sKRF[1], sKRF[1].count)])

    if stop == "A2":
        return finish()
    AB = Alloc(RBASE)
    KT = AB.bf16(S, parts=(0, 97))
    V = AB.bf16(NS * 4 * 128).rearrange("p (t c) -> p t c", c=128)
    QT = AB.bf16(SO, parts=(0, 97))
    PT = [AB.bf16(512) for _ in range(4)]
    tPT = [Tok() for _ in range(4)]
    RR = AB.f32(512)
    tRR = Tok()
    OT = AB.f32(512)
    tOT = Tok()
    RRb = [RR, AB.f32(512)]
    OTb = [OT, AB.f32(512)]
    tRRb = [tRR, Tok()]
    tOTb = [tOT, Tok()]
    RBb = [AB.f32(512), AB.f32(512)]
    tRBb = [Tok(), Tok()]
    sRBb = [dsem("d_rb0"), dsem("d_rb1")]
    tRRS = [Tok(), Tok()]
    R1 = AB.f32(512, parts=(64, 96))
    R2 = AB.f32(512, parts=(64, 96))
    tR12 = Tok()
    STG = [AB.f32(512) for _ in range(2)]
    tSTG = [Tok(), Tok()]
    sSTG = [dsem("d_stg0"), dsem("d_stg1")]
    tKT = [Tok() for _ in range(max(NS, PAST // 512 + 1))]
    tV = Tok()
    tQT = Tok()
    mset("pool", QT[96:97, :], 1.0, R=[], W=[tQT])

    exr = [0]

    def ex_bank():
        i = 6 + exr[0] % 2
        exr[0] += 1
        return i

    def rope_rows(kgroups):
        for (c0, n, ti) in kgroups:
            pb = ex_bank()
            mm(psb[pb][0:97, 0:n], SEL[0:33, :], KR[0:33, c0:c0 + n], True, True, R=[tW, tKR[ti]], W=[pst[pb]])
            cp("dve", KT[64:97, c0:c0 + n], psb[pb][64:97, 0:n], R=[pst[pb]], W=[tKT[ti]])

    uc = [0]
    oc = [0]

    tVs = [Tok() for _ in range(max(NS, PAST // 512 + 1))]
    tQTi = [Tok() for _ in range(NO)]

    def ktask(h, sl):
        pb = ex_bank()
        c0 = sl * 512
        mm(psb[pb][0:64, 0:512], WK1[:, h, 0:64], CKV[:, c0:c0 + 512], True, True, R=[tW, tCKV[sl]], W=[pst[pb]])
        cp("dve", KT[0:64, c0:c0 + 512], psb[pb][0:64, 0:512], R=[pst[pb]], W=[tKT[sl]])

    def vtask(h, sl):
        par = h % 2
        voff = 64 * par
        onec = 64 if par == 0 else 0
        pb = ex_bank()
        for ii in range(4):
            c0 = sl * 512 + ii * 128
            mm(psb[pb][:, ii * 64:(ii + 1) * 64], CKV[:, c0:c0 + 128], WV[:, h, :], True, True, R=[tW, tCKV[sl]], W=[pst[pb]])
        cp("dve", V[:, sl * 4:sl * 4 + 4, voff:voff + 64], psb[pb][:, 0:256].rearrange("p (t c) -> p t c", c=64), R=[pst[pb]], W=[tVs[sl]])
        mset("pool", V[:, sl * 4:sl * 4 + 4, onec:onec + 1], 1.0, R=[], W=[tVs[sl]])

    def qtask(h, i):
        c0 = i * 512
        pa = ex_bank()
        pb2 = ex_bank()
        for c in range(2):
            mm(psb[pa][0:96, 0:512], WQ[:, c, h, 0:96], QN[:, c, c0:c0 + 512], c == 0, c == 1, R=[tW, tQN[i]], W=[pst[pa]])
        for c in range(2):
            mm(psb[pb2][0:96, 0:512], WQR[:, c, h, :], QN[:, c, c0:c0 + 512], c == 0, c == 1, R=[tW, tQN[i]], W=[pst[pb2]])
        cp("dve", QT[0:64, c0:c0 + 512], psb[pa][0:64, 0:512], R=[pst[pa]], W=[tQTi[i]])
        tt("dve", R1[:, 0:512], psb[pa][64:96, 0:512], COS[:, c0:c0 + 512], ALU.mult, R=[pst[pa], tCS[i]], W=[tR12])
        tt("dve", R2[:, 0:512], psb[pb2][64:96, 0:512], SIN[:, c0:c0 + 512], ALU.mult, R=[pst[pb2], tCS[i]], W=[tR12])
        tt("pool", QT[64:96, c0:c0 + 512], R1[:, 0:512], R2[:, 0:512], ALU.add, R=[tR12], W=[tQTi[i]])

    def prompt_tiles(h):
        tiles = []
        for i in range(NO - 1, -1, -1):
            units = []
            for sl in list(range(2 * i - 1, -1, -1)) + [NS - 1]:
                for u in range(4):
                    units.append((sl * 512 + u * 128, 128, sl * 4 + u, 0, False, sl))
            for u in range(4):
                units.append((2 * i * 512 + u * 128, 128, 2 * i * 4 + u, 128 * u, True, 2 * i))
            freed = [2 * i, 2 * i - 1] if i >= 1 else [0, NS - 1]
            tiles.append(dict(qc0=i * 512, nq=512, units=units, ti=i, freed=freed, qtok=tQTi[i],
                              gdst=(lambda vr, i=i, h=h: G[vr, h // 2, i * 512:(i + 1) * 512]), gtok=tG[i]))
        return tiles

    def attend_prompt(h, tiles, next_h, extra_tasks=None):
        par = h % 2
        vr = slice(0, 64) if par == 0 else slice(64, 128)
        sr = 64 if par == 0 else 0
        mcols = 65 if par == 0 else 128
        flat = []
        for tl in tiles:
            ob = 4 + oc[0] % 2
            oc[0] += 1
            nu = len(tl["units"])
            for ui, u in enumerate(tl["units"]):
                flat.append((tl, u, ui == 0, ui == nu - 1, ob, ui))
        pend = g_pend
        tasks = list(extra_tasks) if extra_tasks else []
        LA = 3
        idx = 0
        while idx < len(flat) + LA or tasks:
            if idx < len(flat):
                tl, u, first, last, ob, ui_ = flat[idx]
                kc0, nk, vt, qlo, diag, ktok = u
                nq = tl["nq"] - qlo
                sb = uc[0] % 4
                uc[0] += 1
                flat[idx] = flat[idx] + (sb,)
                mm(psb[sb][0:nk, 0:nq], KT[0:97, kc0:kc0 + nk], QT[0:97, tl["qc0"] + qlo:tl["qc0"] + tl["nq"]], True, True,
                   R=[tKT[ktok], tl["qtok"], tQT], W=[pst[sb]])
                act(PT[sb][0:nk, 0:nq], psb[sb][0:nk, 0:nq], AF.Exp, R=[pst[sb]], W=[tPT[sb]], scale=SM_SCALE)
                if diag:
                    mset("pool", PT[sb][64:128, 0:64], 0.0, R=[], W=[tPT[sb]])
            j = idx - LA
            if 0 <= j < len(flat):
                tl, u, first, last, ob, ui_, sb = flat[j]
                kc0, nk, vt, qlo, diag, ktok = u
                nq = tl["nq"] - qlo
                mm(psb[ob][0:mcols, qlo:tl["nq"]], V[0:nk, vt, 0:mcols], PT[sb][0:nk, 0:nq], first, last, R=[tVs[ktok], tPT[sb]], W=[pst[ob]])
                for fn_ in tl.get("posts", {}).get(ui_, []):
                    tasks.append(fn_)
                if last:
                    nqt = tl["nq"]
                    act(RRb[ob - 4][sr:sr + 1, 0:nqt], psb[ob][sr:sr + 1, 0:nqt], AF.Ln, R=[pst[ob]], W=[tRRb[ob - 4]])
                    act(RRb[ob - 4][sr:sr + 1, 0:nqt], RRb[ob - 4][sr:sr + 1, 0:nqt], AF.Exp, R=[], W=[tRRb[ob - 4]], scale=-1.0)
                    cp("dve", OTb[ob - 4][vr, 0:nqt], psb[ob][vr, 0:nqt], R=[pst[ob]], W=[tOTb[ob - 4]])
                    if BCAST_DMA:
                        k_ = ob - 4
                        dma("sp", rrs[k_:k_ + 1, 0:nqt], RRb[k_][sr:sr + 1, 0:nqt], R=[tRRb[k_]], W=[tRRS[k_]], sem=sRBb[k_])
                        dma("sp", RBb[k_][vr, 0:nqt], rrs[k_:k_ + 1, 0:nqt].partition_broadcast(64),
                            R=[tRRS[k_]], W=[tRBb[k_]], sem=sRBb[k_])
                    pend.append((g_cnt[0] + (14 if BCAST_DMA else 9), tl, ob, nqt, vr))
                    if next_h is not None and "freed" in tl:
                        for sl in tl["freed"]:
                            tasks.append(lambda sl=sl: ktask(next_h, sl))
                            tasks.append(lambda sl=sl: vtask(next_h, sl))
                        tasks.append(lambda i=tl["ti"]: qtask(next_h, i))
            if tasks and (idx % 2 == 0 or idx >= len(flat)):
                tasks.pop(0)()
            g_cnt[0] += 1
            flush_pend(g_cnt[0])
            idx += 1

    g_pend = []
    g_cnt = [0]

    def flush_pend(upto):
        if True:
            while g_pend and g_pend[0][0] <= upto:
                _, tl, ob, nqt, vr = g_pend.pop(0)
                sr = 64 if vr.start == 0 else 0
                if BCAST_DMA:
                    tt("dve", tl["gdst"](vr), OTb[ob - 4][vr, 0:nqt], RBb[ob - 4][vr, 0:nqt], ALU.mult,
                       R=[tOTb[ob - 4], tRBb[ob - 4]], W=[tl["gtok"]])
                else:
                    bb = ex_bank()
                    mm(psb[bb][:, 0:nqt], ONESF[sr:sr + 1, :], RRb[ob - 4][sr:sr + 1, 0:nqt], True, True, R=[tW, tRRb[ob - 4]], W=[pst[bb]])
                    tt("dve", tl["gdst"](vr), OTb[ob - 4][vr, 0:nqt], psb[bb][vr, 0:nqt], ALU.mult, R=[tOTb[ob - 4], pst[bb]], W=[tl["gtok"]])

    rope_rows([(s_ * 512, 512, s_) for s_ in range(NS)])
    for sl in range(NS):
        ktask(0, sl)
        vtask(0, sl)
    for i in range(NO):
        qtask(0, i)
    NKG = PAST // 512
    cache_tasks = []
    for g in range(NKG):
        def _t(g=g):
            i = g % 2
            dma("sp", STG[i], cckvT[:, g * 512:(g + 1) * 512], R=[], W=[tSTG[i]], sem=sSTG[i])
            cp("dve", CKV[:, g * 512:(g + 1) * 512], STG[i], R=[tSTG[i]], W=[tCKV[g]])
        cache_tasks.append(_t)
    for g in range(NKG):
        def _t(g=g):
            i = g % 2
            dma("sp", STG[i][0:32, :], ckrT[:, g * 512:(g + 1) * 512], R=[], W=[tSTG[i]], sem=sSTG[i])
            cp("dve", KR[0:32, g * 512:(g + 1) * 512], STG[i][0:32, :], R=[tSTG[i]], W=[tKR[g]])
        cache_tasks.append(_t)
    for h in range(NH):
        attend_prompt(h, prompt_tiles(h), h + 1 if h + 1 < NH else None, extra_tasks=cache_tasks if h == NH - 1 else None)
    flush_pend(1 << 60)
    P.barrier()

    if stop == "B":
        return finish()
    mset("pool", KR[32:33, 0:PAST + T], 0.0, R=[], W=[tKR[g] for g in range(NKG + 1)])
    cp("pool", CKV[:, PAST:PAST + T], CKVN[:, 0:T], R=[tSMP], W=[tCKV[NKG]])
    cp("pool", KR[0:32, PAST:PAST + T], KRN[:, 0:T], R=[tSMP], W=[tKR[NKG]])
    kgroups_s = [(g * 512, 512, g) for g in range(NKG)] + [(PAST, T, NKG)]
    tQTs = [Tok(), Tok()]

    def ktask_s(h, g):
        c0, n, _ = kgroups_s[g]
        pb = ex_bank()
        mm(psb[pb][0:64, 0:n], WK1[:, h, 0:64], CKV[:, c0:c0 + n], True, True, R=[tW, tCKV[g]], W=[pst[pb]])
        cp("dve", KT[0:64, c0:c0 + n], psb[pb][0:64, 0:n], R=[pst[pb]], W=[tKT[g]])

    def vtask_s(h, g):
        par = h % 2
        voff = 64 * par
        onec = 64 if par == 0 else 0
        pb = ex_bank()
        if g < NKG:
            for ii in range(4):
                c0 = g * 512 + ii * 128
                mm(psb[pb][:, ii * 64:(ii + 1) * 64], CKV[:, c0:c0 + 128], WV[:, h, :], True, True, R=[tW, tCKV[g]], W=[pst[pb]])
            cp("dve", V[:, g * 4:g * 4 + 4, voff:voff + 64], psb[pb][:, 0:256].rearrange("p (t c) -> p t c", c=64), R=[pst[pb]], W=[tVs[g]])
            mset("pool", V[:, g * 4:g * 4 + 4, onec:onec + 1], 1.0, R=[], W=[tVs[g]])
        else:
            vt = PAST // 128
            mm(psb[pb][0:T, 0:64], CKV[:, PAST:PAST + T], WV[:, h, :], True, True, R=[tW, tCKV[g]], W=[pst[pb]])
            cp("dve", V[0:T, vt, voff:voff + 64], psb[pb][0:T, 0:64], R=[pst[pb]], W=[tVs[g]])
            mset("pool", V[0:T, vt:vt + 1, onec:onec + 1], 1.0, R=[], W=[tVs[g]])

    def qtask_s(h):
        c0 = (h % 2) * 64
        pa = ex_bank()
        pb2 = ex_bank()
        for c in range(2):
            mm(psb[pa][0:96, 0:T], WQ[:, c, h, 0:96], QNS[:, c, 0:T], c == 0, c == 1, R=[tW, tSMP], W=[pst[pa]])
        for c in range(2):
            mm(psb[pb2][0:96, 0:T], WQR[:, c, h, :], QNS[:, c, 0:T], c == 0, c == 1, R=[tW, tSMP], W=[pst[pb2]])
        cp("dve", QT[0:64, c0:c0 + T], psb[pa][0:64, 0:T], R=[pst[pa]], W=[tQTs[h % 2]])
        tt("dve", R1[:, 0:T], psb[pa][64:96, 0:T], COSS[:, 0:T], ALU.mult, R=[pst[pa], tSMP], W=[tR12])
        tt("dve", R2[:, 0:T], psb[pb2][64:96, 0:T], SINS[:, 0:T], ALU.mult, R=[pst[pb2], tSMP], W=[tR12])
        tt("dve", QT[64:96, c0:c0 + T], R1[:, 0:T], R2[:, 0:T], ALU.add, R=[tR12], W=[tQTs[h % 2]])

    def sample_tile(h, nxt):
        units = []
        posts = {}
        for g in range(NKG + 1):
            if g < NKG:
                for u in range(4):
                    units.append((g * 512 + u * 128, 128, g * 4 + u, 0, False, g))
            else:
                units.append((PAST, T, PAST // 128, 0, False, NKG))
            if nxt is not None:
                posts[len(units) - 1] = [lambda g=g: ktask_s(nxt, g), lambda g=g: vtask_s(nxt, g)]
        if nxt is not None:
            posts.setdefault(0, []).insert(0, lambda: qtask_s(nxt))
        return dict(qc0=(h % 2) * 64, nq=T, units=units, posts=posts, qtok=tQTs[h % 2],
                    gdst=(lambda vr, h=h: GS[vr, h // 2, 0:T]), gtok=tGS)

    rope_rows(kgroups_s)
    for g in range(NKG + 1):
        ktask_s(0, g)
        vtask_s(0, g)
    qtask_s(0)
    for h in range(NH):
        attend_prompt(h, [sample_tile(h, h + 1 if h + 1 < NH else None)], None)
    flush_pend(1 << 60)
    P.barrier()

    if stop == "C":
        return finish()
    AD = Alloc(DBASE)
    W1Z = AD.bf16(8 * 1024).rearrange("p (k n) -> p k n", k=8)
    WO1 = AD.bf16(8 * 1024).rearrange("p (k n) -> p k n", k=8)
    tWD = Tok()
    XDp = AD.p
    STD = [arena[:, XDp + i * 4096:XDp + (i + 1) * 4096] for i in range(2)]
    tSTD = [Tok(), Tok()]
    sSTD = [dsem("d_std0"), dsem("d_std1")]
    XD = [AD.f32(4096).rearrange("p (k n) -> p k n", k=8) for _ in range(2)]
    tXD = [Tok(), Tok()]
    sXD = [dsem("d_xd0"), dsem("d_xd1")]
    XND = [AD.bf16(4096).rearrange("p (k n) -> p k n", k=8) for _ in range(2)]
    tXND = [Tok(), Tok()]
    sXND = [dsem("d_xnd0"), dsem("d_xnd1")]
    ZD = [AD.bf16(512) for _ in range(2)]
    tZD = [Tok(), Tok()]
    SQD = AD.bf16(4096).rearrange("p (k n) -> p k n", k=8)
    tSQD = Tok()
    RSD = AD.f32(512)
    tRSD = Tok()

    for q in range(4):
        i = q % 2
        dma("sp", STD[i][:, 0:2048].rearrange("p (c n) -> p c n", c=2),
            w_in1[q * 256:(q + 1) * 256, 416:1440].rearrange("(c p) n -> p c n", p=128), R=[], W=[tSTD[i]], sem=sSTD[i])
        for cc in range(2):
            c = q * 2 + cc
            ts("dve", W1Z[:, c, :], STD[i][:, cc * 1024:(cc + 1) * 1024], g1[:, c:c + 1], None, ALU.mult, None,
               R=[tSTD[i], tVEC], W=[tWD])
    for q in range(2):
        i = q % 2
        dma("sp", STD[i].rearrange("p (c n) -> p c n", c=4), w_out1[q * 512:(q + 1) * 512, :].rearrange("(c p) n -> p c n", p=128),
            R=[], W=[tSTD[i]], sem=sSTD[i])
        for cc in range(4):
            (acp if cc % 2 else (lambda o, a, R, W: cp("dve", o, a, R, W)))(WO1[:, q * 4 + cc, :], STD[i][:, cc * 1024:(cc + 1) * 1024], R=[tSTD[i]], W=[tWD])

    P.barrier()
    dtiles = []
    for i in range(NO):
        dtiles.append(dict(n=512, x1n=x1ns[:, 2 * i * 512:(2 * i + 1) * 512], x1=x1s[:, i * 512:(i + 1) * 512],
                           g=(lambda c, i=i: G[:, c, i * 512:(i + 1) * 512]), gtok=tG[i], y=yT[:, i * 512:(i + 1) * 512]))
    dtiles.append(dict(n=T, x1n=x1ns[:, S:S + T], x1=x1s[:, SO:SO + T], g=(lambda c: GS[:, c, 0:T]), gtok=tGS, y=ysT[:, :]))
    zc = [0]

    def d_load_xnd(di):
        dt_ = dtiles[di]
        n = dt_["n"]
        b = di % 2
        dma("sp", XND[b][:, :, 0:n], dt_["x1n"].rearrange("(c p) n -> p c n", p=128), R=[tX1NS], W=[tXND[b]], sem=sXND[b])

    def d_load_xd(di):
        dt_ = dtiles[di]
        n = dt_["n"]
        b = di % 2
        dma("sp", XD[b][:, :, 0:n], dt_["x1"].rearrange("(c p) n -> p c n", p=128), R=[tX1S], W=[tXD[b]], sem=sXD[b])

    def d_load(di):
        d_load_xnd(di)
        d_load_xd(di)

    def d_zproj(di):
        dt_ = dtiles[di]
        n = dt_["n"]
        b = di % 2
        for c in range(8):
            pb = ps_next()
            for k in range(8):
                mm(psb[pb][:, 0:n], W1Z[:, k, c * 128:(c + 1) * 128], XND[b][:, k, 0:n], k == 0, k == 7, R=[tWD, tXND[b]], W=[pst[pb]])
            zb = zc[0] % 2
            zc[0] += 1
            act(ZD[zb][:, 0:n], psb[pb][:, 0:n], AF.Silu, R=[pst[pb]], W=[tZD[zb]])
            tt("dve", dt_["g"](c), dt_["g"](c), ZD[zb][:, 0:n], ALU.mult, R=[tZD[zb]], W=[dt_["gtok"]])

    def d_outproj(di):
        dt_ = dtiles[di]
        n = dt_["n"]
        b = di % 2
        for dch in range(8):
            pb = ps_next()
            for c in range(8):
                mm(psb[pb][:, 0:n], WO1[:, c, dch * 128:(dch + 1) * 128], dt_["g"](c), c == 0, c == 7, R=[tWD, dt_["gtok"]], W=[pst[pb]])
            tt("dve", XD[b][:, dch, 0:n], psb[pb][:, 0:n], XD[b][:, dch, 0:n], ALU.add, R=[pst[pb]], W=[tXD[b]])
        act(SQD[:, :, 0:n], XD[b][:, :, 0:n], AF.Square, R=[tXD[b]], W=[tSQD])

    def d_epilogue(di):
        dt_ = dtiles[di]
        n = dt_["n"]
        b = di % 2
        pss = sumsq(lambda c: SQD[:, c, 0:n], 8, n, tSQD)
        rstd_from_ps(pss, n, 1.0 / D, RSD, tRSD)
        for dch in range(8):
            stt("dve", XD[b][:, dch, 0:n], XD[b][:, dch, 0:n], gf[:, dch:dch + 1], RSD[:, 0:n], ALU.mult, ALU.mult,
                R=[tRSD, tVEC], W=[tXD[b]])
        dma("sp", dt_["y"].rearrange("(c p) n -> p c n", p=128), XD[b][:, :, 0:n], R=[tXD[b]], W=[tOUT], sem=sXD[b])

    d_load(0)
    if len(dtiles) > 1:
        d_load(1)
    d_zproj(0)
    for di in range(len(dtiles)):
        d_outproj(di)
        if di + 1 < len(dtiles):
            d_zproj(di + 1)
        if di + 2 < len(dtiles):
            d_load_xnd(di + 2)
        d_epilogue(di)
        if di + 2 < len(dtiles):
            d_load_xd(di + 2)

    return finish()


def _pc(v, k):
    return np.ascontiguousarray(v.reshape(k, 128).T)


def _pck(a):
    return np.ascontiguousarray(a.reshape(8, 128, 2).transpose(1, 0, 2).reshape(128, 16))


def _unpck(a):
    return np.ascontiguousarray(np.asarray(a).reshape(128, 8, 2).transpose(1, 0, 2).reshape(1024, 2).T)


def make_in_maps(inputs, S, PAST, T, n_cores):
    f32 = np.float32
    xp = np.asarray(inputs["x_prompt"], f32)
    xs = np.asarray(inputs["x_sample"], f32)
    sc = np.asarray(inputs["state_conv"], f32)
    cckv = np.asarray(inputs["cache_ckv"], f32)
    ckr = np.asarray(inputs["cache_krope"], f32)
    NS = S // 512
    inv = (1.0 / (np.float32(10000.0) ** (np.arange(0, 32, 2, dtype=f32) / np.float32(32)))).astype(f32)
    sel = np.zeros((33, 97), f32)
    for r in range(33):
        sel[r, 64 + r] = 1.0
    common = dict(
        sel=sel,
        w_in0=np.ascontiguousarray(inputs["conv_w_in"][0], f32),
        w_out0=np.ascontiguousarray(inputs["conv_w_out"][0], f32),
        w_in1=np.ascontiguousarray(inputs["mla_w_in"][0], f32),
        w_qb=np.ascontiguousarray(inputs["mla_w_qb"][0], f32),
        w_kvb=np.ascontiguousarray(inputs["mla_w_kvb"][0], f32),
        w_out1=np.ascontiguousarray(inputs["mla_w_out"][0], f32),
    )
    ng = np.asarray(inputs["norm_g"], f32)
    vec_base = np.zeros((128, 64), f32)
    vec_base[:, 0:8] = _pc(ng[0], 8)
    vec_base[:, 8:16] = _pc(ng[1], 8)
    vec_base[:, 16:24] = _pc(np.asarray(inputs["final_norm_g"], f32), 8)
    cwm = np.asarray(inputs["conv_w"], f32)[0]
    for k in range(3):
        vec_base[:, 24 + k:48:3] = _pc(cwm[k], 8)
    vec_base[:, 48:50] = _pc(np.asarray(inputs["mla_q_norm_g"], f32)[0], 2)
    vec_base[:, 50] = np.asarray(inputs["mla_kv_norm_g"], f32)[0]
    for p in range(96):
        vec_base[p, 51] = inv[(p % 32) % 16]
        vec_base[p, 53] = -1.0 if (p % 32) < 16 else 1.0
    maps = []
    for c in range(n_cores):
        b, par = c // 2, c % 2
        xb = xp[b]
        if par:
            xb = np.roll(xb, -512, axis=0)
            pos = np.roll(np.arange(S), -512)
            xh = xp[b, 510:512]
        else:
            pos = np.arange(S)
            xh = np.zeros((2, D), f32)
        pos = np.concatenate([pos, PAST + np.arange(T)]).astype(f32)
        kb = np.zeros((1, S), f32)
        if par == 0:
            kb[0, (NS - 1) * 512:] = NEG
        vec = vec_base.copy()
        vec[:, 52] = 1.0 if par == 0 else 0.0
        m = dict(common)
        m.update(
            xT=np.ascontiguousarray(xb.T), xhT=_pck(xh.T), xsT=np.ascontiguousarray(xs[c].T),
            scT=_pck(sc[0, c].T), cckvT=np.ascontiguousarray(cckv[0, c].T),
            ckrT=np.ascontiguousarray(ckr[0, c].T), posr=np.ascontiguousarray(np.broadcast_to(pos[None, :], (96, S + T))),
            kbias=kb, vecs=vec,
        )
        maps.append(m)
    return maps


def assemble(results, S, T, n_cores):
    f32 = np.float32
    NB = n_cores // 2
    NS = S // 512
    y_p = np.zeros((NB, S, D), f32)
    y_s = np.zeros((n_cores, T, D), f32)
    conv_p = np.zeros((1, NB, 2, D), f32)
    ckv_p = np.zeros((1, NB, S, 128), f32)
    kr_p = np.zeros((1, NB, S, 32), f32)
    conv_s = np.zeros((1, n_cores, 2, D), f32)
    ckv_s = np.zeros((1, n_cores, T, 128), f32)
    kr_s = np.zeros((1, n_cores, T, 32), f32)
    for c in range(n_cores):
        r = results[c]
        b, par = c // 2, c % 2
        yt = np.asarray(r["yT"])
        for i in range(NS // 2):
            g = 2 * i + par
            y_p[b, g * 512:(g + 1) * 512, :] = yt[:, i * 512:(i + 1) * 512].T
        y_s[c] = np.asarray(r["ysT"]).T
        conv_s[0, c] = _unpck(r["convs"])
        ckv_s[0, c] = np.asarray(r["ckvsT"]).T
        kr_s[0, c] = np.asarray(r["krsT"]).T
        if par == 0:
            conv_p[0, b] = _unpck(r["convp"])
            ckv_p[0, b] = np.asarray(r["ckvpT"]).T
            kr_p[0, b] = np.asarray(r["krpT"]).T
    return (y_p, y_s, conv_p, ckv_p, kr_p, conv_s, ckv_s, kr_s)


def kernel(**inputs):
    S, PAST, T, n = 8192, 4096, 64, 8
    nc = build(S, PAST, T)
    maps = make_in_maps(inputs, S, PAST, T, n)
    res = run_bass_kernel_spmd(nc, maps, core_ids=list(range(n)))
    return assemble(res.results, S, T, n)
```
